# Optimizing a Trainium2 kernel written in Bass

```python
import math
import jax
import jax.numpy as jnp
from jax import lax
import numpy as np

D_MODEL = 1024
BATCH = 16
SEQ = 256
DEPTH = 2
DEC_BATCH = 8
DEC_SEQ = 2048
PAST_LEN = 256

GRID_W = 64
NA_HEADS = 16
HEAD_DIM = 64
NA_WIDTH = NA_HEADS * HEAD_DIM
WIN_R = 8
WIN_C = 16
SSD_INNER = 2 * D_MODEL
SSD_HEADDIM = 64
SSD_HEADS = SSD_INNER // SSD_HEADDIM
SSD_GROUPS = 4
D_STATE = 128
SSD_CONV = 4
CHUNK = 128
CONV_DIM = SSD_INNER + 2 * SSD_GROUPS * D_STATE
D_FF = 2816
FFN_CONV = 3
Q_BLOCK = 128
SPLIT_SIZES = (NA_WIDTH, NA_WIDTH, NA_WIDTH, SSD_INNER, CONV_DIM, 2 * SSD_HEADS, D_MODEL, D_MODEL)
P_IN = sum(SPLIT_SIZES)
EPS = 1e-6

kernel_name = 'hybrid_na_ssd_prefix_dit_step'


def _split(x, sizes):
    out, start = [], 0
    for s in sizes:
        out.append(x[..., start:start + s])
        start += s
    return out


def _rmsnorm(x, g):
    xf = x.astype(jnp.float32)
    y = xf * lax.rsqrt(jnp.mean(xf * xf, axis=-1, keepdims=True) + EPS)
    return y.astype(x.dtype) * g


def _dwconv(x, w, b):
    k, ch = w.shape
    left = k // 2
    y = lax.conv_general_dilated(x, w[:, None, :], window_strides=(1,), padding=[(left, k - 1 - left)],
                                 dimension_numbers=('NWC', 'WIO', 'NWC'), feature_group_count=ch)
    return y + b


def _adaln(cond, lp):
    m = jax.nn.silu(cond) @ lp['w_ada'] + lp['b_ada']
    return _split(m, (D_MODEL,) * 6)


def _mixer_inputs(x, mod, lp):
    bsz, t, _ = x.shape
    h = _rmsnorm(x, lp['norm1_g']) * (1 + mod[1]) + mod[0]
    q, k, v, z, xbc, dt_raw, ga, gb = _split(h @ lp['w_in'], SPLIT_SIZES)
    q = _rmsnorm(q.reshape(bsz, t, NA_HEADS, HEAD_DIM), lp['q_norm_g'])
    k = _rmsnorm(k.reshape(bsz, t, NA_HEADS, HEAD_DIM), lp['k_norm_g'])
    v = v.reshape(bsz, t, NA_HEADS, HEAD_DIM)
    return q, k, v, z, xbc, dt_raw, ga, gb


def _context_attention(q, k, v):
    bsz, length, _, _ = q.shape
    nb = length // Q_BLOCK
    qb = jnp.moveaxis(q.reshape(bsz, nb, Q_BLOCK, NA_HEADS, HEAD_DIM), 1, 0)
    scale = HEAD_DIM ** -0.5

    def block(qi):
        s = jnp.einsum('bqhd,bkhd->bhqk', qi, k).astype(jnp.float32) * scale
        p = jax.nn.softmax(s, axis=-1).astype(v.dtype)
        return jnp.einsum('bhqk,bkhd->bqhd', p, v)

    o = lax.map(block, qb)
    return jnp.moveaxis(o, 0, 1).reshape(bsz, length, NA_WIDTH)


def _neighbourhood_attention(q, k, v, k_ctx, v_ctx, rpb):
    bsz, t, _, _ = q.shape
    rows = t // GRID_W
    kr = min(WIN_R, rows)
    qg = q.reshape(bsz, rows, GRID_W, NA_HEADS, HEAD_DIM)
    kg = k.reshape(bsz, rows, GRID_W, NA_HEADS, HEAD_DIM)
    vg = v.reshape(bsz, rows, GRID_W, NA_HEADS, HEAD_DIM)
    cols = jnp.arange(GRID_W)
    col_start = jnp.clip(cols - WIN_C // 2, 0, GRID_W - WIN_C)
    col_mask = (cols[None, :] >= col_start[:, None]) & (cols[None, :] < col_start[:, None] + WIN_C)
    col_idx = jnp.clip(cols[None, :] - cols[:, None] + WIN_C - 1, 0, 2 * WIN_C - 2)
    col_bias = rpb[:, :, col_idx].astype(jnp.float32)
    scale = HEAD_DIM ** -0.5

    def row_block(r):
        rs = jnp.clip(r - kr // 2, 0, rows - kr)
        q_r = lax.dynamic_index_in_dim(qg, r, axis=1, keepdims=False)
        k_r = lax.dynamic_slice_in_dim(kg, rs, kr, axis=1)
        v_r = lax.dynamic_slice_in_dim(vg, rs, kr, axis=1)
        s_win = jnp.einsum('bqhd,bjkhd->bhqjk', q_r, k_r).astype(jnp.float32) * scale
        row_idx = rs + jnp.arange(kr) - r + WIN_R - 1
        bias = jnp.transpose(col_bias[:, row_idx], (0, 2, 1, 3))
        s_win = jnp.where(col_mask[:, None, :], s_win + bias, -jnp.inf)
        s_ctx = jnp.einsum('bqhd,bkhd->bhqk', q_r, k_ctx).astype(jnp.float32) * scale
        s = jnp.concatenate([s_win.reshape(bsz, NA_HEADS, GRID_W, kr * GRID_W), s_ctx], axis=-1)
        p = jax.nn.softmax(s, axis=-1).astype(v.dtype)
        p_win = p[..., :kr * GRID_W].reshape(bsz, NA_HEADS, GRID_W, kr, GRID_W)
        p_ctx = p[..., kr * GRID_W:]
        return jnp.einsum('bhqjk,bjkhd->bqhd', p_win, v_r) + jnp.einsum('bhqk,bkhd->bqhd', p_ctx, v_ctx)

    o = lax.map(row_block, jnp.arange(rows))
    return jnp.moveaxis(o, 0, 1).reshape(bsz, t, NA_WIDTH)


def _ssd_scan(x, dt, a, bm, cm, h0):
    bsz, t, nh, hp = x.shape
    nc = t // CHUNK
    hg = nh // SSD_GROUPS
    f32 = jnp.float32
    xc = x.astype(f32).reshape(bsz, nc, CHUNK, SSD_GROUPS, hg, hp)
    dtc = dt.reshape(bsz, nc, CHUNK, SSD_GROUPS, hg)
    bc = bm.astype(f32).reshape(bsz, nc, CHUNK, SSD_GROUPS, D_STATE)
    cc = cm.astype(f32).reshape(bsz, nc, CHUNK, SSD_GROUPS, D_STATE)
    acs = jnp.cumsum(dtc * a.reshape(SSD_GROUPS, hg), axis=2)
    causal = jnp.tril(jnp.ones((CHUNK, CHUNK), dtype=bool))[None, None, :, :, None, None]
    seg = acs[:, :, :, None] - acs[:, :, None, :]
    lmat = jnp.exp(jnp.where(causal, seg, -jnp.inf))
    cb = jnp.einsum('bcign,bcjgn->bcijg', cc, bc)
    w = cb[..., None] * lmat * dtc[:, :, None]
    y_diag = jnp.einsum('bcijgh,bcjghp->bcighp', w, xc)
    xw = xc * (jnp.exp(acs[:, :, -1:] - acs) * dtc)[..., None]
    states = jnp.einsum('bcjgn,bcjghp->bcghpn', bc, xw)
    chunk_decay = jnp.exp(acs[:, :, -1])

    def step(h, inp):
        st, dec = inp
        return h * dec[..., None, None] + st, h

    h_init = h0.astype(f32).reshape(bsz, SSD_GROUPS, hg, hp, D_STATE)
    h_final, h_prev = lax.scan(step, h_init, (jnp.moveaxis(states, 1, 0), jnp.moveaxis(chunk_decay, 1, 0)))
    h_prev = jnp.moveaxis(h_prev, 0, 1)
    y_off = jnp.einsum('bcign,bcghpn->bcighp', cc, h_prev) * jnp.exp(acs)[..., None]
    y = (y_diag + y_off).reshape(bsz, t, nh, hp).astype(x.dtype)
    return y, h_final.reshape(bsz, nh, hp, D_STATE).astype(x.dtype)


def _ssd_branch(z, xbc, dt_raw, lp, h0_f, h0_b):
    bsz, t, _ = z.shape
    xbc = jax.nn.silu(_dwconv(xbc, lp['ssd_conv_w'], lp['ssd_conv_b']))
    xs, bm, cm = _split(xbc, (SSD_INNER, SSD_GROUPS * D_STATE, SSD_GROUPS * D_STATE))
    xs = xs.reshape(bsz, t, SSD_HEADS, SSD_HEADDIM)
    bm = bm.reshape(bsz, t, SSD_GROUPS, D_STATE)
    cm = cm.reshape(bsz, t, SSD_GROUPS, D_STATE)
    a = -jnp.exp(lp['a_log'].astype(jnp.float32))
    dt = jax.nn.softplus(dt_raw.astype(jnp.float32).reshape(bsz, t, 2, SSD_HEADS) + lp['dt_bias'].astype(jnp.float32))
    y_f, h_f = _ssd_scan(xs, dt[:, :, 0], a[0], bm, cm, h0_f)
    y_b, h_b = _ssd_scan(xs[:, ::-1], dt[:, ::-1, 1], a[1], bm[:, ::-1], cm[:, ::-1], h0_b)
    y = y_f + y_b[:, ::-1] + lp['d_skip'][:, None] * xs
    y = _rmsnorm(y.reshape(bsz, t, SSD_INNER) * jax.nn.silu(z), lp['ssd_norm_g'])
    return y, h_f, h_b


def _finish_layer(x, mod, lp, na, ssd, ga, gb):
    mix = jax.nn.sigmoid(ga) * (na @ lp['w_na_out']) + jax.nn.sigmoid(gb) * (ssd @ lp['w_ssd_out'])
    x = x + mod[2] * (mix @ lp['w_o'])
    h2 = _rmsnorm(x, lp['norm2_g']) * (1 + mod[4]) + mod[3]
    u = _dwconv(h2 @ lp['w_up'], lp['ffn_conv_w'], lp['ffn_conv_b'])
    val, gate = _split(u, (D_FF, D_FF))
    return x + mod[5] * ((jax.nn.silu(gate) * val) @ lp['w_down'])


def _context_layer(x, c_ctx, lp):
    mod = _adaln(c_ctx[None, None, :], lp)
    q, k, v, z, xbc, dt_raw, ga, gb = _mixer_inputs(x, mod, lp)
    na = _context_attention(q, k, v)
    h0 = jnp.zeros((x.shape[0], SSD_HEADS, SSD_HEADDIM, D_STATE), x.dtype)
    ssd, h_f, h_b = _ssd_branch(z, xbc, dt_raw, lp, h0, h0)
    return _finish_layer(x, mod, lp, na, ssd, ga, gb), k, v, h_f, h_b


def _latent_layer(x, c, k_ctx, v_ctx, h0_f, h0_b, lp):
    mod = _adaln(c[:, None, :], lp)
    q, k, v, z, xbc, dt_raw, ga, gb = _mixer_inputs(x, mod, lp)
    na = _neighbourhood_attention(q, k, v, k_ctx, v_ctx, lp['rpb'])
    ssd, _, _ = _ssd_branch(z, xbc, dt_raw, lp, h0_f, h0_b)
    return _finish_layer(x, mod, lp, na, ssd, ga, gb)


def setup_inputs(seed: int = 0) -> dict:
    key = jax.random.key(seed)
    ks = jax.random.split(key, 32)
    f32 = jnp.float32

    def nrm(k, shape, s):
        return jax.random.normal(k, shape, f32) * s

    L = DEPTH
    dt0 = jnp.exp(jax.random.uniform(ks[18], (L, 2, SSD_HEADS), f32, math.log(1e-3), math.log(1e-1)))
    return {
        'x_prompt': nrm(ks[0], (BATCH, SEQ, D_MODEL), 1.0),
        'x_sample': nrm(ks[1], (DEC_BATCH, DEC_SEQ, D_MODEL), 1.0),
        'c': nrm(ks[2], (DEC_BATCH, D_MODEL), 1.0),
        'cache_k': nrm(ks[3], (DEC_BATCH, L, PAST_LEN, NA_HEADS, HEAD_DIM), 1.0),
        'cache_v': nrm(ks[4], (DEC_BATCH, L, PAST_LEN, NA_HEADS, HEAD_DIM), 0.5),
        'state_ssd_fwd': nrm(ks[5], (DEC_BATCH, L, SSD_HEADS, SSD_HEADDIM, D_STATE), 0.1),
        'state_ssd_bwd': nrm(ks[6], (DEC_BATCH, L, SSD_HEADS, SSD_HEADDIM, D_STATE), 0.1),
        'c_ctx': nrm(ks[7], (D_MODEL,), 1.0),
        'w_ada': nrm(ks[8], (L, D_MODEL, 6 * D_MODEL), 0.5 * D_MODEL ** -0.5),
        'b_ada': nrm(ks[9], (L, 6 * D_MODEL), 0.02),
        'norm1_g': 1.0 + nrm(ks[10], (L, D_MODEL), 0.01),
        'w_in': nrm(ks[11], (L, D_MODEL, P_IN), D_MODEL ** -0.5),
        'q_norm_g': 1.0 + nrm(ks[12], (L, HEAD_DIM), 0.01),
        'k_norm_g': 1.0 + nrm(ks[13], (L, HEAD_DIM), 0.01),
        'rpb': nrm(ks[14], (L, NA_HEADS, 2 * WIN_R - 1, 2 * WIN_C - 1), 0.1),
        'ssd_conv_w': nrm(ks[15], (L, SSD_CONV, CONV_DIM), SSD_CONV ** -0.5),
        'ssd_conv_b': nrm(ks[16], (L, CONV_DIM), 0.02),
        'a_log': jnp.log(jax.random.uniform(ks[17], (L, 2, SSD_HEADS), f32, 1.0, 16.0)),
        'dt_bias': dt0 + jnp.log(-jnp.expm1(-dt0)),
        'd_skip': 1.0 + nrm(ks[19], (L, SSD_HEADS), 0.1),
        'ssd_norm_g': 1.0 + nrm(ks[20], (L, SSD_INNER), 0.01),
        'w_na_out': nrm(ks[21], (L, NA_WIDTH, D_MODEL), NA_WIDTH ** -0.5),
        'w_ssd_out': nrm(ks[22], (L, SSD_INNER, D_MODEL), SSD_INNER ** -0.5),
        'w_o': nrm(ks[23], (L, D_MODEL, D_MODEL), D_MODEL ** -0.5),
        'norm2_g': 1.0 + nrm(ks[24], (L, D_MODEL), 0.01),
        'w_up': nrm(ks[25], (L, D_MODEL, 2 * D_FF), D_MODEL ** -0.5),
        'ffn_conv_w': nrm(ks[26], (L, FFN_CONV, 2 * D_FF), FFN_CONV ** -0.5),
        'ffn_conv_b': nrm(ks[27], (L, 2 * D_FF), 0.02),
        'w_down': nrm(ks[28], (L, D_FF, D_MODEL), D_FF ** -0.5),
    }


def reference(x_prompt, x_sample, c, cache_k, cache_v, state_ssd_fwd, state_ssd_bwd, c_ctx,
              w_ada, b_ada, norm1_g, w_in, q_norm_g, k_norm_g, rpb, ssd_conv_w, ssd_conv_b,
              a_log, dt_bias, d_skip, ssd_norm_g, w_na_out, w_ssd_out, w_o, norm2_g,
              w_up, ffn_conv_w, ffn_conv_b, w_down):
    xp = x_prompt
    xs = x_sample
    new_k, new_v, new_hf, new_hb = [], [], [], []
    for l in range(DEPTH):
        lp = {
            'w_ada': w_ada[l], 'b_ada': b_ada[l], 'norm1_g': norm1_g[l], 'w_in': w_in[l],
            'q_norm_g': q_norm_g[l], 'k_norm_g': k_norm_g[l], 'rpb': rpb[l],
            'ssd_conv_w': ssd_conv_w[l], 'ssd_conv_b': ssd_conv_b[l], 'a_log': a_log[l],
            'dt_bias': dt_bias[l], 'd_skip': d_skip[l], 'ssd_norm_g': ssd_norm_g[l],
            'w_na_out': w_na_out[l], 'w_ssd_out': w_ssd_out[l], 'w_o': w_o[l], 'norm2_g': norm2_g[l],
            'w_up': w_up[l], 'ffn_conv_w': ffn_conv_w[l], 'ffn_conv_b': ffn_conv_b[l], 'w_down': w_down[l],
        }
        xp, k_l, v_l, hf_l, hb_l = _context_layer(xp, c_ctx, lp)
        new_k.append(k_l)
        new_v.append(v_l)
        new_hf.append(hf_l)
        new_hb.append(hb_l)
        xs = _latent_layer(xs, c, cache_k[:, l], cache_v[:, l], state_ssd_fwd[:, l], state_ssd_bwd[:, l], lp)
    new_cache_k = jnp.stack(new_k, axis=1)
    new_cache_v = jnp.stack(new_v, axis=1)
    new_state_ssd_fwd = jnp.stack(new_hf, axis=1)
    new_state_ssd_bwd = jnp.stack(new_hb, axis=1)
    return (xp, xs, new_cache_k, new_cache_v, new_state_ssd_fwd, new_state_ssd_bwd)
```

```python
import contextlib
import numpy as np
import concourse.bass as bass
import concourse.mybir as mybir
from concourse.bass_utils import run_bass_kernel_spmd

F32 = mybir.dt.float32
BF16 = mybir.dt.bfloat16
AF = mybir.ActivationFunctionType
ALU = mybir.AluOpType

EPOCH = 30000
NDMASEM = 12

DEPTH = 2
NT = 20
NW = 5
TOK = 2560
Q0, K0, V0, Z0, X0, B0, C0, DT0, GA0, GB0 = 0, 1024, 2048, 3072, 5120, 7168, 7680, 8192, 8256, 9280
PIN = 10304
DFF = 2816
SEQS = [(0, [0, 1]), (1, [2, 3]), (2, list(range(4, 20)))]
EPS = 1e-6
SEGCOL = [(0, 256, 2), (256, 512, 260), (512, 2560, 518)]
UCW = 2568


class Op:
    __slots__ = ("eng", "fn", "reads", "writes", "dma", "deps", "sig", "idx", "dsem", "dval", "n")

    def __init__(self, eng, fn, reads, writes, dma):
        self.eng = eng
        self.fn = fn
        self.reads = reads
        self.writes = writes
        self.dma = dma
        self.deps = []
        self.sig = False
        self.dsem = None
        self.dval = 0
        self.n = 0


class Prog:
    ENGS = ("pe", "act", "dve", "pool", "sp")

    def __init__(self):
        self.ops = []
        self.fences = []

    def fence(self):
        if not self.fences or self.fences[-1] != len(self.ops):
            self.fences.append(len(self.ops))

    def add(self, eng, fn, reads=(), writes=(), dma=False):
        op = Op(eng, fn, tuple(reads), tuple(writes), dma)
        op.idx = len(self.ops)
        self.ops.append(op)
        return op

    def schedule(self):
        ops = self.ops
        last_w = {}
        readers = {}
        fences = list(self.fences)
        fi = 0
        fence_deps = []
        seen_fence = {e: 0 for e in self.ENGS}
        last_comp = {}
        pend_dma = []
        for op in ops:
            while fi < len(fences) and fences[fi] <= op.idx:
                fence_deps = list(last_comp.values()) + pend_dma
                pend_dma = []
                fi += 1
            deps = set()
            for r in op.reads:
                w = last_w.get(r)
                if w is not None:
                    deps.add(w)
            for w_ in op.writes:
                w = last_w.get(w_)
                if w is not None:
                    deps.add(w)
                for rd in readers.get(w_, ()):
                    deps.add(rd)
            for r in op.reads:
                readers.setdefault(r, []).append(op.idx)
            for w_ in op.writes:
                last_w[w_] = op.idx
                readers[w_] = []
            deps.discard(op.idx)
            keep = []
            best = {}
            for d in deps:
                p = ops[d]
                if p.dma:
                    keep.append(d)
                    continue
                if p.eng == op.eng and not op.dma and op.eng == "pe":
                    raw = any(r in p.writes for r in op.reads)
                    if not raw:
                        continue
                b = best.get(p.eng)
                if b is None or d > b:
                    best[p.eng] = d
            keep.extend(best.values())
            if seen_fence[op.eng] < fi:
                seen_fence[op.eng] = fi
                for d in fence_deps:
                    p = ops[d]
                    if (not p.dma) and p.eng == op.eng and not op.dma:
                        continue
                    keep.append(d)
            if op.dma:
                pend_dma.append(op.idx)
            else:
                last_comp[op.eng] = op.idx
            op.deps = keep
            for d in keep:
                ops[d].sig = True
        cnt = {e: 0 for e in self.ENGS}
        dcnt = {}
        dprev = {}
        dptr = {e: 0 for e in self.ENGS}
        for op in ops:
            if op.dma:
                s = (op.eng, dptr[op.eng] % NDMASEM)
                dptr[op.eng] += 1
                prev = dprev.get(s)
                if prev is not None:
                    op.deps.append(prev)
                dprev[s] = op.idx
                dcnt[s] = dcnt.get(s, 0) + 16
                op.dsem = s
                op.dval = dcnt[s]
            elif op.sig:
                cnt[op.eng] += 1
                op.n = cnt[op.eng]
        self.cnt = cnt
        self.dcnt = dcnt

    def emit(self, nc):
        self.schedule()
        ops = self.ops
        with contextlib.ExitStack() as st:
            esems = {}
            for e in self.ENGS:
                k = (self.cnt[e] + EPOCH - 1) // EPOCH
                esems[e] = [st.enter_context(nc.semaphore(f"s_{e}_{i}")) for i in range(max(k, 1))]
            dsems = {}
            for s in self.dcnt:
                dsems[s] = st.enter_context(nc.semaphore(f"d_{s[0]}_{s[1]}"))
            block = st.enter_context(nc.Block())
            per_eng = {e: [o for o in ops if o.eng == e] for e in self.ENGS}

            def make(e):
                def body(eng):
                    known = {x: 0 for x in self.ENGS}
                    dknown = {}
                    for op in per_eng[e]:
                        needc = {}
                        needd = {}
                        for d in op.deps:
                            p = ops[d]
                            if p.dma:
                                if p.dval > needd.get(p.dsem, 0):
                                    needd[p.dsem] = p.dval
                            elif p.n > needc.get(p.eng, 0):
                                needc[p.eng] = p.n
                        for ds, dv in needd.items():
                            if dknown.get(ds, 0) >= dv:
                                continue
                            eng.wait_ge(dsems[ds], dv)
                            dknown[ds] = dv
                        for pe_, n_ in needc.items():
                            if known[pe_] >= n_:
                                continue
                            eng.wait_ge(esems[pe_][(n_ - 1) // EPOCH], (n_ - 1) % EPOCH + 1)
                            known[pe_] = n_
                        ins = op.fn(eng)
                        if op.dma:
                            ins.then_inc(dsems[op.dsem], 16)
                        elif op.sig:
                            ins.then_inc(esems[e][(op.n - 1) // EPOCH], 1)
                    if e == "sp":
                        for s, v in self.dcnt.items():
                            if dknown.get(s, 0) < v:
                                eng.wait_ge(dsems[s], v)
                return body

            block.tensor(make("pe"))
            block.scalar(make("act"))
            block.vector(make("dve"))
            block.gpsimd(make("pool"))
            block.sync(make("sp"))


class KB:
    def __init__(self):
        self.nc = bass.Bass("TRN2", target_bir_lowering=False)
        self.P = Prog()
        self.ins = {}
        self.n_uid = 0

    def din(self, name, shape):
        ap = self.nc.dram_tensor(name, list(shape), F32, kind="ExternalInput").ap()
        self.ins[name] = tuple(shape)
        return ap

    def dout(self, name, shape):
        return self.nc.dram_tensor(name, list(shape), F32, kind="ExternalOutput").ap()

    def dscr(self, name, shape, dt):
        if DBG:
            return self.nc.dram_tensor(name, list(shape), dt, kind="ExternalOutput").ap()
        return self.nc.dram_tensor(name, list(shape), dt).ap()

    def mm(self, out, lhsT, rhs, start, stop, R, W):
        self.P.add("pe", lambda e: e.matmul(out, lhsT=lhsT, rhs=rhs, start=start, stop=stop), R, W)

    def tr(self, out, in_, ident, R, W):
        self.P.add("pe", lambda e: e.transpose(out=out, in_=in_, identity=ident), R, W)

    def act(self, out, in_, func, R, W, scale=None, bias=None, accum=None):
        kw = {}
        if scale is not None:
            kw["scale"] = scale
        if bias is not None:
            kw["bias"] = bias
        if accum is not None:
            kw["accum_out"] = accum
        self.P.add("act", lambda e: e.activation(out=out, in_=in_, func=func, **kw), R, W)

    def tt(self, out, in0, in1, op, R, W, eng="dve"):
        self.P.add(eng, lambda e: e.tensor_tensor(out=out, in0=in0, in1=in1, op=op), R, W)

    def ts(self, out, in0, s1, s2, op0, op1, R, W, eng="dve"):
        if s2 is None:
            self.P.add(eng, lambda e: e.tensor_scalar(out=out, in0=in0, scalar1=s1, scalar2=None, op0=op0), R, W)
        else:
            self.P.add(eng, lambda e: e.tensor_scalar(out=out, in0=in0, scalar1=s1, scalar2=s2, op0=op0, op1=op1), R, W)

    def stt(self, out, in0, scalar, in1, op0, op1, R, W, eng="dve"):
        self.P.add(eng, lambda e: e.scalar_tensor_tensor(out=out, in0=in0, scalar=scalar, in1=in1, op0=op0, op1=op1), R, W)

    def cp(self, out, in_, R, W, eng="dve"):
        self.P.add(eng, lambda e: e.tensor_copy(out=out, in_=in_), R, W)

    def recip(self, out, in_, R, W):
        self.P.add("dve", lambda e: e.reciprocal(out=out, in_=in_), R, W)

    def memset(self, ap, val, W, eng="dve"):
        self.P.add(eng, lambda e: e.memset(ap, val), (), W)

    def dma(self, out, in_, R, W, q="sp"):
        self.P.add(q, lambda e: e.dma_start(out=out, in_=in_), R, W, dma=True)

    def uid(self, s):
        self.n_uid += 1
        return (s, self.n_uid)


def bcast_heads(ap2, nh, p):
    return ap2.unsqueeze(2).to_broadcast([128, nh, p])


STOP = None
DBG = False


class _Stop(Exception):
    pass


def build():
    kb = KB()
    try:
        _build_inner(kb)
    except _Stop:
        pass
    return kb


def _build_inner(kb):
    nc = kb.nc
    P = kb.P
    APc = None

    xin = kb.din("xin", [TOK, 1024])
    cvecT = kb.din("cvecT", [128, 8, 2])
    ck = kb.din("ck", [DEPTH, 256, 1024])
    cv = kb.din("cv", [DEPTH, 256, 1024])
    stf = kb.din("stf", [DEPTH, 2048, 128])
    stb = kb.din("stb", [DEPTH, 2048, 128])
    w_ada = kb.din("w_ada", [DEPTH, 1024, 6144])
    b_ada = kb.din("b_ada", [DEPTH, 6144])
    norm1_g = kb.din("norm1_g", [DEPTH, 1024])
    w_in = kb.din("w_in", [DEPTH, 1024, PIN])
    gqk_d = kb.din("gqk", [128, 4])
    rpbG = kb.din("rpbG", [DEPTH, 16, 64, 15, 64])
    cw_d = kb.din("cw", [DEPTH, 128, 24, 4])
    cb_d = kb.din("cb", [DEPTH, 128, 24])
    a_log = kb.din("a_log", [DEPTH, 64])
    dt_bias = kb.din("dt_bias", [DEPTH, 64])
    d_skip = kb.din("d_skip", [DEPTH, 32])
    gn_d = kb.din("gn", [DEPTH, 128, 16])
    w_na_out = kb.din("w_na_out", [DEPTH, 1024, 1024])
    w_ssd_out = kb.din("w_ssd_out", [DEPTH, 2048, 1024])
    w_o = kb.din("w_o", [DEPTH, 1024, 1024])
    norm2_g = kb.din("norm2_g", [DEPTH, 1024])
    w_up = kb.din("w_up", [DEPTH, 1024, 2 * DFF])
    fw_d = kb.din("fw", [DEPTH, 128, 44, 3])
    fb_d = kb.din("fb", [DEPTH, 128, 44])
    w_down = kb.din("w_down", [DEPTH, DFF, 1024])
    c_ident = kb.din("c_ident", [128, 128])
    c_tri = kb.din("c_tri", [128, 4, 128])
    c_bd = kb.din("c_bd", [128, 128])
    c_onesz = kb.din("c_onesz", [128, 2, 128])
    c_mask = kb.din("c_mask", [128, 2, 22, 64])

    yout = kb.dout("yout", [TOK, 1024])
    nk = kb.dout("nk", [2, DEPTH, 256, 1024])
    nv = kb.dout("nv", [2, DEPTH, 256, 1024])
    nsf = kb.dout("nsf", [2, DEPTH, 2048, 128])
    nsb = kb.dout("nsb", [2, DEPTH, 2048, 128])

    modd = kb.dscr("modd", [DEPTH, 2, 6144], F32)
    x1d = kb.dscr("x1d", [TOK, 1024], F32)
    x2d = kb.dscr("x2d", [TOK, 1024], F32)
    naT_d = kb.dscr("naT_d", [8, 128, TOK], BF16)
    yzT_d = kb.dscr("yzT_d", [16, 128, TOK], BF16)
    gT_d = kb.dscr("gT_d", [16, 128, TOK], BF16)
    h2T_d = kb.dscr("h2T_d", [8, 128, TOK], BF16)

    def chk(n, l=0):
        if float(n) == int(n):
            P.fence()
        if STOP == n and l == 0:
            P.emit(nc)
            raise _Stop()

    def wview(w2d, c0, n):
        return w2d.rearrange("(k p) c -> p k c", p=128)[:, :, c0:c0 + n]

    with contextlib.ExitStack() as top:
        def sbt(st, name, shape, dt):
            kb.n_uid += 1
            return st.enter_context(nc.sbuf_tensor(f"{name}_s{kb.n_uid}", list(shape), dt))

        def pst(st, name, shape, dt):
            kb.n_uid += 1
            return st.enter_context(nc.psum_tensor(f"{name}_p{kb.n_uid}", list(shape), dt))

        ident_f = sbt(top, "ident_f", [128, 128], F32)
        ident_b = sbt(top, "ident_b", [128, 128], BF16)
        tri_f = sbt(top, "tri_f", [128, 4, 128], F32)
        tri_b = sbt(top, "tri_b", [128, 4, 128], BF16)
        ones_f = sbt(top, "ones_f", [128, 128], F32)
        bd_b = sbt(top, "bd_b", [128, 128], BF16)
        onesz_b = sbt(top, "onesz_b", [128, 2, 128], BF16)
        scT = sbt(top, "scT", [128, 8, 2], BF16)
        gqk = sbt(top, "gqk", [128, 4], F32)
        APc = type(ident_f[:])

        kb.dma(ident_f[:], c_ident, [], ["ident_f"])
        kb.dma(tri_f[:], c_tri, [], ["tri_f"])
        kb.dma(ident_b[:], c_ident, [], ["ident_b"], q="pool")
        kb.dma(tri_b[:], c_tri, [], ["tri_b"], q="pool")
        kb.dma(bd_b[:], c_bd, [], ["bd_b"], q="pool")
        kb.dma(onesz_b[:], c_onesz, [], ["onesz_b"], q="pool")
        kb.dma(gqk[:], gqk_d, [], ["gqk"])
        kb.memset(ones_f[:], 1.0, ["ones_f"])
        kb.ts(gqk[:, 0:1], gqk[:, 0:1], 0.125, None, ALU.mult, None, ["gqk"], ["gqk"])
        kb.ts(gqk[:, 2:3], gqk[:, 2:3], 0.125, None, ALU.mult, None, ["gqk"], ["gqk"])

        def ph0_gen(st, l):
            wada = [sbt(st, f"wada{i}", [128, 8, 512], BF16) for i in range(2)]
            bad = [sbt(st, f"bad{i}", [2, 512], F32) for i in range(2)]
            mst = [sbt(st, f"mst{i}", [2, 512], F32) for i in range(2)]
            pm = [pst(st, f"pm{i}", [128, 512], F32) for i in range(2)]
            for ct in range(12):
                b = ct % 2
                kb.dma(wada[b][:], wview(w_ada[l], ct * 512, 512), [], [("wada", b)], q="pool")
                kb.dma(bad[b][:], b_ada[l:l + 1, ct * 512:(ct + 1) * 512].to_broadcast([2, 512]), [], [("bad", b)])
                for k in range(8):
                    kb.mm(pm[b][0:2, :], scT[:, k, :], wada[b][:, k, :], k == 0, k == 7,
                          ["scT", ("wada", b)], [("pm", b)])
                kb.tt(mst[b][:], pm[b][0:2, :], bad[b][:], ALU.add, [("pm", b), ("bad", b)], [("mst", b)])
                kb.dma(modd[l, :, ct * 512:(ct + 1) * 512], mst[b][:], [("mst", b)], [("modd", l)])
                yield

        with contextlib.ExitStack() as st:
            csT = sbt(st, "csT", [128, 8, 2], F32)
            kb.dma(csT[:], cvecT, [], ["csT"])
            kb.act(scT[:], csT[:], AF.Silu, ["csT"], ["scT"])
            for _ in ph0_gen(st, 0):
                pass

        chk(0)

        def mod_row(l, r, idx):
            return modd[l, r:r + 1, idx * 1024:(idx + 1) * 1024].to_broadcast([128, 1024])

        def norm_tile(xt, xkey, s_bc, sh_bc, bckeys, junk, ss, tmp, hb, pT, dest, destkey, b, tmpkey=None):
            tmpkey = tmpkey or ("ntmp", b)
            kb.act(junk[:], xt, AF.Square, [xkey], [("njunk",), ("nss", b)], accum=ss[:, 0:1])
            kb.act(ss[:, 1:2], ss[:, 0:1], AF.Ln, [("nss", b)], [("nss", b)], scale=1.0 / 1024, bias=EPS)
            kb.act(ss[:, 2:3], ss[:, 1:2], AF.Exp, [("nss", b)], [("nss", b)], scale=-0.5)
            kb.stt(tmp[:], xt, ss[:, 2:3], s_bc[:], ALU.mult, ALU.mult, [xkey, ("nss", b)] + bckeys, [tmpkey])
            kb.tt(hb[:], tmp[:], sh_bc[:], ALU.add, [tmpkey] + bckeys, [("nhb", b)])
            for c in range(8):
                kb.tr(pT[:, c, :], hb[:, c * 128:(c + 1) * 128], ident_b[:], [("nhb", b), "ident_b"], [("npT", b)])
            kb.act(dest, pT[:], AF.Copy, [("npT", b)], [destkey])

        for l in range(DEPTH):
            xsrc = xin if l == 0 else x2d
            xfin = x2d if l == 0 else yout
            xskey = "x2d" if l == 1 else "xin"
            xfkey = "x2d" if l == 0 else "yout"
            with contextlib.ExitStack() as Lst:
                rstd_tok = sbt(Lst, "rstd_tok", [128, NT], F32)
                Hst = contextlib.ExitStack()
                hT = sbt(Hst, "hT", [128, 8, TOK], BF16)
                with contextlib.ExitStack() as st:
                    s_bc = [sbt(st, f"s_bc{r}", [128, 1024], F32) for r in range(2)]
                    sh_bc = [sbt(st, f"sh_bc{r}", [128, 1024], F32) for r in range(2)]
                    g1 = sbt(st, "g1", [128, 1024], F32)
                    xt = [sbt(st, f"xt{i}", [128, 1024], F32) for i in range(2)]
                    junk = sbt(st, "junk", [128, 1024], BF16)
                    tmp = [sbt(st, f"tmp{i}", [128, 1024], F32) for i in range(2)]
                    hb = [sbt(st, f"hb{i}", [128, 1024], BF16) for i in range(2)]
                    ss = [sbt(st, f"ss{i}", [128, 4], F32) for i in range(2)]
                    pT = [pst(st, f"pT{i}", [128, 8, 128], BF16) for i in range(2)]
                    kb.dma(g1[:], norm1_g[l:l + 1, :].to_broadcast([128, 1024]), [], ["g1"])
                    for r in range(2):
                        kb.dma(s_bc[r][:], mod_row(l, r, 1), [("modd", l)], [("s_bc", r)])
                        kb.dma(sh_bc[r][:], mod_row(l, r, 0), [("modd", l)], [("sh_bc", r)])
                        kb.stt(s_bc[r][:], s_bc[r][:], 1.0, g1[:], ALU.add, ALU.mult, [("s_bc", r), "g1"], [("s_bc", r)])
                    for t in range(NT):
                        b = t % 2
                        r = 0 if t < 4 else 1
                        kb.dma(xt[b][:], xsrc[t * 128:(t + 1) * 128, :], [(xskey, t)], [("xt", b)])
                        norm_tile(xt[b][:], ("xt", b), s_bc[r], sh_bc[r], [("s_bc", r), ("sh_bc", r)], junk, ss[b],
                                  tmp[b], hb[b], pT[b], hT[:, :, t * 128:(t + 1) * 128], ("hT", t), b)

                chk(1, l)

                def hTw(w):
                    return [("hT", t) for t in range(w * 4, w * 4 + 4)]

                with contextlib.ExitStack() as st:
                    wq = [sbt(st, f"wq{i}", [128, 8, 128], BF16) for i in range(2)]
                    wk = [sbt(st, f"wk{i}", [128, 8, 128], BF16) for i in range(2)]
                    wv = [sbt(st, f"wv{i}", [128, 8, 128], BF16) for i in range(2)]
                    qn = [sbt(st, f"qn{i}", [128, TOK], BF16) for i in range(2)]
                    kn = [sbt(st, f"kn{i}", [128, TOK], BF16) for i in range(2)]
                    Vz = [sbt(st, f"Vz{i}", [128, NT, 2, 128], BF16) for i in range(2)]
                    sq = [sbt(st, f"sq{i}", [128, 512], BF16) for i in range(2)]
                    rs = [sbt(st, f"rs{i}", [128, 512], F32) for i in range(2)]
                    kcT = sbt(st, "kcT", [128, 8, 256], BF16)
                    Vcz = sbt(st, "Vcz", [128, 2, 16, 128], BF16)
                    ckst = sbt(st, "ckst", [128, 1024], F32)
                    ckb = sbt(st, "ckb", [128, 1024], BF16)
                    TBraw = [sbt(st, f"TBraw{i}", [128, 22, 64], F32) for i in range(2)]
                    TBe = [sbt(st, "TBe0", [128, 22, 64], F32)] * 2
                    TB = [[sbt(st, f"TB{i}_{kd}", [128, 22, 64], BF16) for kd in range(2)] for i in range(4)]
                    masks = sbt(st, "masks", [128, 2, 22, 64], F32)
                    E = [sbt(st, f"E{i}", [128, 512], BF16) for i in range(4)]
                    rD = [sbt(st, f"rD{i}", [128, 512], F32) for i in range(2)]
                    naS = [sbt(st, f"naS{i}", [128, TOK], BF16) for i in range(2)]
                    kout = [sbt(st, "kout0", [128, 4, 128], F32)] * 2
                    vout = [sbt(st, "vout0", [128, 4, 128], F32)] * 2
                    pq = [pst(st, f"pq{i}", [128, 512], F32) for i in range(2)]
                    pss = pst(st, "pss", [128, 512], F32)
                    pS = [pst(st, f"pS{i}", [128, 512], F32) for i in range(2)]
                    pOx = [pst(st, "pOe", [128, 512], F32), pst(st, "pOo", [128, 512], F32)]
                    pTb = pst(st, "pTb", [128, 1024], BF16)

                    kb.dma(masks[:], c_mask, [], ["masks"])
                    for i in range(2):
                        kb.memset(TBraw[i][:], 0.0, [("TBraw", i)])
                        kb.memset(Vz[i][:], 1.0, [("Vz", i)])
                    kb.memset(Vcz[:], 1.0, ["Vcz"])
                    for kt in range(2):
                        kb.dma(ckst[:], ck[l, kt * 128:(kt + 1) * 128, :], [], ["ckst"])
                        kb.cp(ckb[:], ckst[:], ["ckst"], ["ckb"])
                        for hp in range(8):
                            kb.tr(pTb[:, hp * 128:(hp + 1) * 128], ckb[:, hp * 128:(hp + 1) * 128], ident_b[:],
                                  ["ckb", "ident_b"], ["pTb"])
                        kb.act(kcT[:, :, kt * 128:(kt + 1) * 128], pTb[:].rearrange("p (h t) -> p h t", h=8), AF.Copy,
                               ["pTb"], ["kcT"])
                    for kt in range(2):
                        kb.dma(ckst[:], cv[l, kt * 128:(kt + 1) * 128, :], [], ["ckst"])
                        src = ckst[:].rearrange("p (h e d) -> p h e d", e=2, d=64)
                        dstv = Vcz[:, kt].rearrange("p (h e) d -> p h e d", e=2)
                        for e in range(2):
                            kb.cp(dstv[:, :, e, e * 64:(e + 1) * 64], src[:, :, e, :], ["ckst"], ["Vcz"])


                    def win_items(n):
                        js = [range(0, 6), range(2, 10), range(6, 14), range(10, 16)][n]
                        out = []
                        for j in js:
                            i0 = 10 - 2 * j + 8 * n
                            if n == 0:
                                if j <= 3:
                                    segs, q = [(0, 4, 0), (4, 8, 1)], (0, 8)
                                else:
                                    segs, q = [(4, 8, 1)], (4, 8)
                            elif n == 3:
                                if j >= 12:
                                    segs, q = [(0, 5, 1), (5, 8, 0)], (0, 8)
                                else:
                                    segs, q = [(0, 5, 1)], (0, 5)
                            else:
                                segs, q = [(0, 8, 1)], (0, 8)
                            out.append((j, i0, q, segs))
                        return out

                    def prod_gen(hp):
                        b = hp % 2
                        kb.dma(wq[b][:], wview(w_in[l], Q0 + hp * 128, 128), [], [("wq", b)], q="pool")
                        kb.dma(wk[b][:], wview(w_in[l], K0 + hp * 128, 128), [], [("wk", b)], q="pool")
                        kb.dma(wv[b][:], wview(w_in[l], V0 + hp * 128, 128), [], [("wv", b)], q="pool")
                        for e in range(2):
                            h = 2 * hp + e
                            sl = b * 2 + e
                            kb.dma(TBraw[e][0:64, 3:18, :], rpbG[l, h], [], [("TBraw", e)])
                            kb.dma(TBraw[e][64:128, 4:19, :], rpbG[l, h], [], [("TBraw", e)])
                            kb.act(TBe[e][:], TBraw[e][:], AF.Exp, [("TBraw", e)], [("TBe", 0)])
                            for kd in range(2):
                                kb.tt(TB[sl][kd][:], TBe[e][:], masks[:, kd], ALU.mult, [("TBe", 0), "masks"], [("TB", sl)])
                            yield
                        tiles = [(wt, dst, gcol, key, w) for (wt, dst, gcol, key) in
                                 ((wq, qn, l * 2 + 0, "qn"), (wk, kn, l * 2 + 1, "kn")) for w in range(NW)]

                        def stepA(i):
                            wt, dst, gcol, key, w = tiles[i]
                            wkey = ("wq", b) if key == "qn" else ("wk", b)
                            pb = i % 2
                            for k in range(8):
                                kb.mm(pq[pb][:], wt[b][:, k, :], hT[:, k, w * 512:(w + 1) * 512], k == 0, k == 7,
                                      [wkey] + hTw(w), [("pq", pb)])
                            kb.act(sq[pb][:], pq[pb][:], AF.Square, [("pq", pb)], [("sq", pb)])

                        def stepB(i):
                            wt, dst, gcol, key, w = tiles[i]
                            pb = i % 2
                            kb.mm(pss[:], bd_b[:], sq[pb][:], True, True, [("sq", pb), "bd_b"], ["pss"])
                            kb.act(rs[pb][:], pss[:], AF.Ln, ["pss"], [("rs", pb)], scale=1.0 / 64, bias=EPS)
                            kb.act(rs[pb][:], rs[pb][:], AF.Exp, [("rs", pb)], [("rs", pb)], scale=-0.5)
                            kb.stt(dst[b][:, w * 512:(w + 1) * 512], pq[pb][:], gqk[:, gcol:gcol + 1], rs[pb][:],
                                   ALU.mult, ALU.mult, [("pq", pb), ("rs", pb), "gqk"], [(key, b, w)])

                        for i in range(len(tiles) + 1):
                            if i < len(tiles):
                                stepA(i)
                            if i >= 1:
                                stepB(i - 1)
                            yield
                        for t0 in range(0, NT, 4):
                            pb = (t0 // 4) % 2
                            for ti in range(4):
                                t = t0 + ti
                                for k in range(8):
                                    kb.mm(pq[pb][:, ti * 128:(ti + 1) * 128], hT[:, k, t * 128:(t + 1) * 128], wv[b][:, k, :],
                                          k == 0, k == 7, [("wv", b), ("hT", t)], [("pq", pb)])
                            base = Vz[b][:]
                            ps0 = base.ap[0][0]
                            dst = APc(base.tensor, base.offset + t0 * 256, [[ps0, 128], [256, 4], [192, 2], [1, 64]])
                            kb.act(dst, pq[pb][:].rearrange("p (t e d) -> p t e d", t=4, e=2), AF.Copy,
                                   [("pq", pb)], [("Vz", b)])
                            if t0 == 0:
                                kb.act(vout[b][:], pq[pb][:].rearrange("p (t d) -> p t d", t=4), AF.Copy, [("pq", pb)], [("vout", 0)])
                                for s in range(2):
                                    kb.dma(nv[s, l].rearrange("(t p) d -> p t d", p=128)[:, :, hp * 128:(hp + 1) * 128],
                                           vout[b][:, 2 * s:2 * s + 2, :], [("vout", 0)], [("nv", s, l, hp)])
                            yield
                        for t in range(4):
                            kb.tr(pTb[:, t * 128:(t + 1) * 128], kn[b][:, t * 128:(t + 1) * 128], ident_b[:],
                                  [("kn", b, 0), "ident_b"], ["pTb"])
                        kb.cp(kout[b][:], pTb[:, 0:512].rearrange("p (t d) -> p t d", t=4), ["pTb"], [("kout", 0)])
                        for s in range(2):
                            kb.dma(nk[s, l].rearrange("(t p) d -> p t d", p=128)[:, :, hp * 128:(hp + 1) * 128],
                                   kout[b][:, 2 * s:2 * s + 2, :], [("kout", 0)], [("nk", s, l, hp)])
                        yield

                    ecnt = 0
                    scnt = 0
                    for _ in prod_gen(0):
                        pass
                    for hp in range(8):
                        b = hp % 2
                        nxt = prod_gen(hp + 1) if hp < 7 else None
                        tickn = [0]

                        def tick():
                            tickn[0] += 1
                            if nxt is not None and tickn[0] % 4 == 0:
                                next(nxt, None)

                        citems = [(s_, e, kt) for s_ in range(2) for e in range(2) for kt in range(2)]
                        cst = [None] * len(citems)

                        def cS(ii):
                            nonlocal scnt, ecnt
                            s_, e, kt = citems[ii]
                            base = s_ * 256
                            hr = slice(e * 64, e * 64 + 64)
                            sb_ = scnt % 2
                            scnt += 1
                            eb = ecnt % 4
                            ecnt += 1
                            kb.mm(pS[sb_][:, 0:256], kn[b][hr, base + kt * 128:base + (kt + 1) * 128],
                                  qn[b][hr, base:base + 256], True, True, [("qn", b, 0), ("kn", b, 0)], [("pS", sb_)])
                            kb.act(E[eb][:, 0:256], pS[sb_][:, 0:256], AF.Exp, [("pS", sb_)], [("E", eb)])
                            cst[ii] = eb

                        def cPV(ii):
                            s_, e, kt = citems[ii]
                            base = s_ * 256
                            eb = cst[ii]
                            si = ii % 4
                            kb.mm(pOx[e][:, 0:256], Vz[b][:, 2 * s_ + kt, e, :], E[eb][:, 0:256], kt == 0, kt == 1,
                                  [("E", eb), ("Vz", b)], [("pO", e)])
                            if si == 3:
                                kb.act(rD[s_][0:64, 0:256], pOx[0][64:128, 0:256], AF.Ln, [("pO", 0)], [("rD", s_)])
                                kb.act(rD[s_][0:64, 0:256], rD[s_][0:64, 0:256], AF.Exp, [("rD", s_)], [("rD", s_)], scale=-1.0)
                                kb.tt(naS[b][0:64, base:base + 256], pOx[0][0:64, 0:256], rD[s_][0:64, 0:256], ALU.mult,
                                      [("pO", 0), ("rD", s_)], [("naS", b)])
                                kb.act(rD[s_][64:128, 0:256], pOx[1][0:64, 0:256], AF.Ln, [("pO", 1)], [("rD", s_)])
                                kb.act(rD[s_][64:128, 0:256], rD[s_][64:128, 0:256], AF.Exp, [("rD", s_)], [("rD", s_)], scale=-1.0)
                                kb.tt(naS[b][64:128, base:base + 256], pOx[1][64:128, 0:256], rD[s_][64:128, 0:256], ALU.mult,
                                      [("pO", 1), ("rD", s_)], [("naS", b)])

                        for ii in range(len(citems) + 2):
                            if ii < len(citems):
                                cS(ii)
                            if ii >= 2:
                                cPV(ii - 2)
                            tick()

                        for n in range(4):
                            qb = 512 + n * 512
                            items = []
                            for e in range(2):
                                items.append((e, "c", 0))
                                for it in win_items(n):
                                    items.append((e, "w", it))
                                items.append((e, "c", 1))
                            LOOK = 3
                            st_ = [None] * len(items)

                            def stageS(ii):
                                nonlocal scnt, ecnt
                                e, kind, it = items[ii]
                                hr = slice(e * 64, e * 64 + 64)
                                sb_ = scnt % 2
                                scnt += 1
                                eb = ecnt % 4
                                ecnt += 1
                                if kind == "c":
                                    lk = kcT[hr, hp, it * 128:(it + 1) * 128]
                                    lkr = ["kcT"]
                                    qlo, qhi = 0, 512
                                    Vl = Vcz[:, it, 2 * hp + e, :]
                                    Vr = ["Vcz"]
                                else:
                                    j, i0, (qa, qz), segs = it
                                    lk = kn[b][hr, 512 + j * 128:512 + (j + 1) * 128]
                                    lkr = [("kn", b, 1 + j // 4)]
                                    qlo, qhi = qa * 64, qz * 64
                                    Vl = Vz[b][:, 4 + j, e, :]
                                    Vr = [("Vz", b)]
                                kb.mm(pS[sb_][:, qlo:qhi], lk, qn[b][hr, qb + qlo:qb + qhi], True, True,
                                      lkr + [("qn", b, 1 + n)], [("pS", sb_)])
                                kb.act(E[eb][:, qlo:qhi], pS[sb_][:, qlo:qhi], AF.Exp, [("pS", sb_)], [("E", eb)])
                                if kind == "w":
                                    for (a, z, kd) in segs:
                                        ev = E[eb][:, a * 64:z * 64].rearrange("p (a c) -> p a c", c=64)
                                        kb.tt(ev, ev, TB[b * 2 + e][kd][:, i0 + a:i0 + z, :], ALU.mult,
                                              [("E", eb), ("TB", b * 2 + e)], [("E", eb)])
                                st_[ii] = (e, eb, qlo, qhi, Vl, Vr)

                            def stagePV(ii):
                                e, eb, qlo, qhi, Vl, Vr = st_[ii]
                                first = ii == 0 or items[ii - 1][0] != e
                                last = ii == len(items) - 1 or items[ii + 1][0] != e
                                kb.mm(pOx[e][:, qlo:qhi], Vl, E[eb][:, qlo:qhi], first, last, [("E", eb)] + Vr, [("pO", e)])

                            for ii in range(len(items) + LOOK):
                                if ii < len(items):
                                    stageS(ii)
                                if ii >= LOOK:
                                    stagePV(ii - LOOK)
                                tick()
                            rb_ = n % 2
                            kb.act(rD[rb_][0:64, :], pOx[0][64:128, :], AF.Ln, [("pO", 0)], [("rD", rb_)])
                            kb.act(rD[rb_][0:64, :], rD[rb_][0:64, :], AF.Exp, [("rD", rb_)], [("rD", rb_)], scale=-1.0)
                            kb.tt(naS[b][0:64, qb:qb + 512], pOx[0][0:64, :], rD[rb_][0:64, :], ALU.mult,
                                  [("pO", 0), ("rD", rb_)], [("naS", b)])
                            kb.act(rD[rb_][64:128, :], pOx[1][0:64, :], AF.Ln, [("pO", 1)], [("rD", rb_)])
                            kb.act(rD[rb_][64:128, :], rD[rb_][64:128, :], AF.Exp, [("rD", rb_)], [("rD", rb_)], scale=-1.0)
                            kb.tt(naS[b][64:128, qb:qb + 512], pOx[1][64:128, :], rD[rb_][64:128, :], ALU.mult,
                                  [("pO", 1), ("rD", rb_)], [("naS", b)])
                        kb.dma(naT_d[hp], naS[b][:], [("naS", b)], [("naT_d", hp)])
                        if nxt is not None:
                            for _ in nxt:
                                pass

                chk(2, l)
                with contextlib.ExitStack() as st:
                    dt_t = sbt(st, "dt_t", [128, NT, 64], F32)
                    da_r = [sbt(st, f"da_r{i}", [128, 64], F32) for i in range(2)]
                    da_b16 = sbt(st, "da_b16", [128, NT, 64], BF16)
                    facin = sbt(st, "facin", [128, NT, 64], F32)
                    dd = sbt(st, "dd", [128, NT, 2, 64], F32)
                    ex1 = [sbt(st, f"ex1_{i}", [128, 64], F32) for i in range(2)]
                    wdt = sbt(st, "wdt", [128, 8, 64], BF16)
                    dtb_bc = sbt(st, "dtb_bc", [128, 64], F32)
                    a_bc = sbt(st, "a_bc", [128, 64], F32)
                    dsk_bc = sbt(st, "dsk_bc", [128, 32], F32)
                    t64 = [sbt(st, f"t64_{i}", [128, 64], F32) for i in range(2)]
                    cwt = sbt(st, "cwt", [128, 24, 4], F32)
                    cbt = sbt(st, "cbt", [128, 24], F32)
                    gnT = sbt(st, "gnT", [128, 16], F32)
                    ssq4 = sbt(st, "ssq4", [128, NT, 4], F32)
                    xs_tok = sbt(st, "xs_tok", [128, NT, 512], BF16)
                    B_tok = sbt(st, "B_tok", [128, NT, 128], BF16)
                    BT = sbt(st, "BT", [128, TOK], BF16)
                    CT = sbt(st, "CT", [128, TOK], BF16)
                    Hbe = sbt(st, "Hbe", [128, NT, 512], BF16)
                    Ucp2 = [sbt(st, f"Ucp{i}", [128, UCW], F32) for i in range(2)]
                    cvt = sbt(st, "cvt", [128, UCW], F32)
                    XT = [sbt(st, "XT0", [128, TOK], BF16)] * 2
                    wblk = [sbt(st, f"wblk{i}", [128, 8, 128], BF16) for i in range(2)]
                    wz = [sbt(st, "wz0", [128, 8, 512], BF16)] * 2
                    szc = [sbt(st, f"szc{i}", [128, 512], BF16) for i in range(2)]
                    DAm = [sbt(st, f"DAm{i}", [128, 4, 128], BF16) for i in range(2)]
                    LTt = [[sbt(st, f"LTt{i}_{j}", [128, 4, 128], BF16) for j in range(4)] for i in range(2)]
                    xdt = [[sbt(st, f"xdt{i}_{j}", [128, 512], BF16) for j in range(2)] for i in range(2)]
                    xwf = [sbt(st, f"xwf{i}", [128, 512], BF16) for i in range(2)]
                    t1 = [sbt(st, f"t1_{i}", [128, 512], BF16) for i in range(2)]
                    xw = t1
                    xsd = [sbt(st, f"xsd{i}", [128, 512], BF16) for i in range(2)]
                    yzb = [sbt(st, f"yzb{i}", [128, 512], BF16) for i in range(2)]
                    cbm = [sbt(st, f"cbm{i}", [128, 2, 128], BF16) for i in range(2)]
                    Hf = sbt(st, "Hf", [128, 512], F32)
                    Hb = sbt(st, "Hb", [128, 512], F32)
                    Hfb = sbt(st, "Hfb", [128, 512], BF16)
                    h0st = sbt(st, "h0st", [128, 4, 128], F32)
                    sto = [h0st] * 2
                    yzTs = [sbt(st, f"yzTs{i}", [128, 4, 128], BF16) for i in range(2)]
                    pp = [pst(st, f"pp{i}", [128, 512], F32) for i in range(2)]
                    pTb = pst(st, "pTbs", [128, 1024], BF16)
                    pseg = [pst(st, f"pseg{i}", [128, 4, 128], F32) for i in range(2)]
                    pY = pst(st, "pY", [128, 512], F32)
                    pZ = [pst(st, f"pZ{i}", [128, 512], F32) for i in range(2)]

                    for i in range(2):
                        kb.memset(Ucp2[i][:], 0.0, [("Ucp", i)])
                    kb.dma(cwt[:], cw_d[l], [], ["cwt"])
                    kb.dma(cbt[:], cb_d[l], [], ["cbt"])
                    kb.dma(gnT[:], gn_d[l], [], ["gnT"])
                    kb.dma(dtb_bc[:], dt_bias[l:l + 1, :].to_broadcast([128, 64]), [], ["dtb_bc"])
                    kb.dma(a_bc[:], a_log[l:l + 1, :].to_broadcast([128, 64]), [], ["a_bc"])
                    kb.dma(dsk_bc[:], d_skip[l:l + 1, :].to_broadcast([128, 32]), [], ["dsk_bc"])
                    kb.act(a_bc[:], a_bc[:], AF.Exp, ["a_bc"], ["a_bc"])
                    kb.ts(a_bc[:], a_bc[:], -1.0, None, ALU.mult, None, ["a_bc"], ["a_bc"])
                    kb.dma(wdt[:], wview(w_in[l], DT0, 64), [], ["wdt"], q="pool")
                    for t in range(NT):
                        b = t % 2
                        for k in range(8):
                            kb.mm(pp[b][:, 0:64], hT[:, k, t * 128:(t + 1) * 128], wdt[:, k, :], k == 0, k == 7,
                                  ["wdt", ("hT", t)], [("pp", b)])
                        kb.tt(t64[b][:], pp[b][:, 0:64], dtb_bc[:], ALU.add, [("pp", b), "dtb_bc"], [("t64", b)])
                        kb.act(t64[b][:], t64[b][:], AF.Exp, [("t64", b)], [("t64", b)])
                        kb.act(dt_t[:, t, :], t64[b][:], AF.Ln, [("t64", b)], [("dt", t)], bias=1.0)
                        kb.tt(da_r[b][:], dt_t[:, t, :], a_bc[:], ALU.mult, [("dt", t), "a_bc"], [("da", b)])
                        kb.cp(da_b16[:, t, :], da_r[b][:], [("da", b)], [("dab", t)])
                        pm3 = pp[b][:, 128:320].rearrange("p (a h) -> p a h", a=3)
                        kb.mm(pm3[:, 0, 0:32], tri_f[:, 0, :], da_r[b][:, 0:32], True, True, [("da", b), "tri_f"], [("pp", b)])
                        kb.mm(pm3[:, 0, 32:64], tri_f[:, 1, :], da_r[b][:, 32:64], True, True, [("da", b), "tri_f"], [("pp", b)])
                        kb.mm(pm3[:, 1, 0:32], tri_f[:, 2, :], da_r[b][:, 0:32], True, True, [("da", b), "tri_f"], [("pp", b)])
                        kb.mm(pm3[:, 1, 32:64], tri_f[:, 3, :], da_r[b][:, 32:64], True, True, [("da", b), "tri_f"], [("pp", b)])
                        kb.mm(pm3[:, 2, :], ones_f[:], da_r[b][:], True, True, [("da", b), "ones_f"], [("pp", b)])
                        kb.act(ex1[b][:], pm3[:, 0, :], AF.Exp, [("pp", b)], [("ex1", b)])
                        kb.act(dd[:, t], pm3[:, 1:3, :], AF.Exp, [("pp", b)], [("dd", t)])
                        kb.tt(facin[:, t, :], ex1[b][:], dt_t[:, t, :], ALU.mult, [("ex1", b), ("dt", t)], [("facin", t)])

                    def fm_block(wtile, wkey, ppi, ub):
                        Ucp = Ucp2[ub]
                        for w in range(NW):
                            pb = (ppi + w) % 2
                            for k in range(8):
                                kb.mm(pp[pb][:], wtile[:, k, :], hT[:, k, w * 512:(w + 1) * 512], k == 0, k == 7,
                                      [wkey] + hTw(w), [("pp", pb)])
                            if w == 0:
                                kb.act(Ucp[:, 2:258], pp[pb][:, 0:256], AF.Copy, [("pp", pb)], [("Ucp", ub)])
                                kb.act(Ucp[:, 260:516], pp[pb][:, 256:512], AF.Copy, [("pp", pb)], [("Ucp", ub)])
                            else:
                                c0 = 518 + (w - 1) * 512
                                kb.act(Ucp[:, c0:c0 + 512], pp[pb][:], AF.Copy, [("pp", pb)], [("Ucp", ub)])

                    def conv4(blk, ub):
                        Ucp = Ucp2[ub]
                        kb.ts(cvt[:, 2:2566], Ucp[:, 2:2566], cwt[:, blk, 2:3], cbt[:, blk:blk + 1], ALU.mult, ALU.add,
                              [("Ucp", ub), "cwt", "cbt"], ["cvt"])
                        for (kk, sh) in ((1, -1), (0, -2), (3, 1)):
                            kb.stt(cvt[:, 2:2566], Ucp[:, 2 + sh:2566 + sh], cwt[:, blk, kk:kk + 1], cvt[:, 2:2566],
                                   ALU.mult, ALU.add, [("Ucp", ub), "cwt", "cvt"], ["cvt"])

                    def silu_out(dst, dkey):
                        for (a, z, c0) in SEGCOL:
                            kb.act(dst[:, a:z], cvt[:, c0:c0 + (z - a)], AF.Silu, ["cvt"], [dkey])

                    wi = 0
                    zi = 0
                    ci_ = 0
                    for g in range(4):
                        blocks = [(X0 + g * 512 + bb * 128, "xs", bb, g * 4 + bb) for bb in range(4)]
                        blocks += [(B0 + g * 128, "B", 0, 16 + g), (C0 + g * 128, "C", 0, 20 + g)]
                        ubs = {}

                        def prodA1(idx):
                            nonlocal wi
                            (col0, kind, bb, blk) = blocks[idx]
                            wb_ = wi % 2
                            wi += 1
                            ubs[idx] = wi % 2
                            kb.dma(wblk[wb_][:], wview(w_in[l], col0, 128), [], [("wblk", wb_)], q="pool")
                            fm_block(wblk[wb_], ("wblk", wb_), wi, wi % 2)

                        def prodA2(idx):
                            (col0, kind, bb, blk) = blocks[idx]
                            conv4(blk, ubs[idx])

                        def prodS(idx):
                            (col0, kind, bb, blk) = blocks[idx]
                            if kind == "C":
                                silu_out(CT, "CT")
                            elif kind == "B":
                                silu_out(BT, "BT")
                            elif idx % 2 == 0:
                                silu_out(XT[0], ("XT", 0))
                            else:
                                silu_out(CT, "CT")

                        def prodB(idx):
                            (col0, kind, bb, blk) = blocks[idx]
                            if kind == "C":
                                return
                            if kind == "B":
                                src, skey = BT, "BT"
                            elif idx % 2 == 0:
                                src, skey = XT[0], ("XT", 0)
                            else:
                                src, skey = CT, "CT"
                            for t0 in (0, 8, 16):
                                nt = min(8, NT - t0)
                                for ti in range(nt):
                                    t = t0 + ti
                                    kb.tr(pTb[:, ti * 128:(ti + 1) * 128], src[:, t * 128:(t + 1) * 128], ident_b[:],
                                          [skey, "ident_b"], ["pTbs"])
                                if kind == "B":
                                    kb.act(B_tok[:, t0:t0 + nt, :], pTb[:, 0:nt * 128].rearrange("p (t c) -> p t c", c=128),
                                           AF.Copy, ["pTbs"], ["B_tok"])
                                else:
                                    kb.act(xs_tok[:, t0:t0 + nt, bb * 128:(bb + 1) * 128],
                                           pTb[:, 0:nt * 128].rearrange("p (t c) -> p t c", c=128), AF.Copy, ["pTbs"], ["xs_tok"])

                        for i_ in range(len(blocks) + 1):
                            if i_ < len(blocks):
                                prodA1(i_)
                            if i_ >= 1:
                                prodS(i_ - 1)
                            if i_ < len(blocks):
                                prodA2(i_)
                            if i_ >= 1:
                                prodB(i_ - 1)
                        zb = zi % 2
                        zi += 1
                        kb.dma(wz[zb][:], wview(w_in[l], Z0 + g * 512, 512), [], [("wz", 0)], q="pool")
                        hs = slice(g * 8, g * 8 + 8)
                        hsb = slice(32 + g * 8, 32 + g * 8 + 8)

                        def xs3(c):
                            return xs_tok[:, c, :].rearrange("p (h d) -> p h d", d=64)

                        def v3(ap):
                            return ap.rearrange("p (h d) -> p h d", d=64)

                        def load_state(srcd, Hdst, hk):
                            kb.dma(h0st[:], srcd[l, g * 512:(g + 1) * 512, :].rearrange("(b p) n -> p b n", p=128),
                                   [], ["h0st"])
                            for bb in range(4):
                                kb.mm(pZ[0][:, bb * 128:(bb + 1) * 128], h0st[:, bb, :], ident_f[:], True, True,
                                      ["h0st", "ident_f"], [("pZ", 0)])
                            kb.cp(Hdst[:], pZ[0][:], [("pZ", 0)], [hk])

                        def store_state(Hsrc, hk, dstd, s_):
                            for bb in range(4):
                                kb.mm(pZ[1][:, bb * 128:(bb + 1) * 128], Hsrc[:, bb * 128:(bb + 1) * 128], ident_f[:],
                                      True, True, [hk, "ident_f"], [("pZ", 1)])
                            stg = Ucp2[0][:, 1024:1536].rearrange("p (b n) -> p b n", b=4)
                            kb.cp(stg, pZ[1][:].rearrange("p (b n) -> p b n", b=4), [("pZ", 1)], [("Ucp", 0)])
                            kb.dma(dstd[s_, l, g * 512:(g + 1) * 512, :].rearrange("(b p) n -> p b n", p=128), stg,
                                   [("Ucp", 0)], [("nstate", hk, s_, l, g)])

                        for (s_, chunks) in SEQS:
                            if s_ < 2:
                                kb.memset(Hb[:], 0.0, ["Hb"])
                            else:
                                load_state(stb, Hb, "Hb")
                            for c in reversed(chunks):
                                xb = ci_ % 2
                                ci_ += 1
                                kb.act(Hbe[:, c, :], Hb[:], AF.Copy, ["Hb"], [("Hbe", c)])
                                kb.tt(v3(xw[xb][:]), xs3(c), bcast_heads(facin[:, c, hsb], 8, 64), ALU.mult,
                                      ["xs_tok", ("facin", c)], [("t1", xb)], eng="pool")
                                kb.mm(pZ[xb][:], B_tok[:, c, :], xw[xb][:], True, True, ["B_tok", ("t1", xb)], [("pZ", xb)])
                                Hb2 = Ucp2[1][:, 1024:1536]
                                kb.tt(v3(Hb2), v3(Hb[:]), bcast_heads(dd[:, c, 1, hsb], 8, 64), ALU.mult,
                                      ["Hb", ("dd", c)], [("Ucp", 1)])
                                kb.tt(Hb[:], Hb2, pZ[xb][:], ALU.add, [("Ucp", 1), ("pZ", xb)], ["Hb"])
                            if s_ < 2:
                                store_state(Hb, "Hb", nsb, s_)

                        seq_first = {chunks[0]: s_ for (s_, chunks) in SEQS}
                        seq_last = {chunks[-1]: s_ for (s_, chunks) in SEQS}
                        if True:
                            def stage1(c):
                                xb = c % 2
                                tokc = slice(c * 128, (c + 1) * 128)
                                for k in range(8):
                                    kb.mm(pp[xb][:], hT[:, k, tokc], wz[zb][:, k, :], k == 0, k == 7,
                                          [("wz", 0), ("hT", c)], [("pp", xb)])
                                kb.act(szc[xb][:], pp[xb][:], AF.Silu, [("pp", xb)], [("szc", xb)])
                                kb.mm(pZ[xb][:, 0:128], BT[:, tokc], CT[:, tokc], True, True, ["BT", "CT"], [("pZ", xb)])
                                kb.tt(cbm[xb][:], pZ[xb][:, 0:128].unsqueeze(1).to_broadcast([128, 2, 128]), tri_b[:, 2:4, :],
                                      ALU.mult, [("pZ", xb), "tri_b"], [("cbm", xb)])
                                for d in range(2):
                                    hsl = hs if d == 0 else hsb
                                    kb.tt(v3(xdt[xb][d][:]), xs3(c), bcast_heads(dt_t[:, c, hsl], 8, 64), ALU.mult,
                                          ["xs_tok", ("dt", c)], [("xdt", xb, d)], eng="pool")
                                kb.tt(v3(xsd[xb][:]), xs3(c), bcast_heads(dsk_bc[:, hs], 8, 64), ALU.mult,
                                      ["xs_tok", "dsk_bc"], [("xsd", xb)])
                                kb.tt(v3(xwf[xb][:]), xs3(c), bcast_heads(facin[:, c, hs], 8, 64), ALU.mult,
                                      ["xs_tok", ("facin", c)], [("xwf", xb)])
                                for d in range(2):
                                    for hq in range(2):
                                        sub = d * 2 + hq
                                        db = sub % 2
                                        h0 = d * 32 + g * 8 + hq * 4
                                        kb.tt(DAm[db][:], tri_b[:, d, :].unsqueeze(1).to_broadcast([128, 4, 128]),
                                              da_b16[:, c, h0:h0 + 4].unsqueeze(2).to_broadcast([128, 4, 128]), ALU.mult,
                                              ["tri_b", ("dab", c)], [("DAm", db)], eng="pool")
                                        for hh in range(4):
                                            kb.mm(pseg[db][:, hh, :], DAm[db][:, hh, :], tri_b[:, 2 + d, :], True, True,
                                                  [("DAm", db), "tri_b"], [("pseg", db)])
                                        kb.act(LTt[xb][sub][:], pseg[db][:], AF.Exp, [("pseg", db)], [("LTt", xb, sub)])
                                        kb.tt(LTt[xb][sub][:], LTt[xb][sub][:],
                                              cbm[xb][:, d, :].unsqueeze(1).to_broadcast([128, 4, 128]),
                                              ALU.mult, [("LTt", xb, sub), ("cbm", xb)], [("LTt", xb, sub)])

                            def stage2(c):
                                xb = c % 2
                                tokc = slice(c * 128, (c + 1) * 128)
                                if c in seq_first:
                                    if seq_first[c] < 2:
                                        kb.memset(Hf[:], 0.0, ["Hf"])
                                    else:
                                        load_state(stf, Hf, "Hf")
                                    kb.act(Hfb[:], Hf[:], AF.Copy, ["Hf"], ["Hfb"])
                                kb.mm(pZ[0][:], CT[:, tokc], Hfb[:], True, True, ["CT", "Hfb"], [("pZ", 0)])
                                kb.tt(v3(t1[0][:]), v3(pZ[0][:]), bcast_heads(dd[:, c, 0, hs], 8, 64), ALU.mult,
                                      [("pZ", 0), ("dd", c)], [("t1", 0)])
                                kb.mm(pZ[1][:], CT[:, tokc], Hbe[:, c, :], True, True, ["CT", ("Hbe", c)], [("pZ", 1)])
                                kb.tt(v3(t1[1][:]), v3(pZ[1][:]), bcast_heads(dd[:, c, 0, hsb], 8, 64), ALU.mult,
                                      [("pZ", 1), ("dd", c)], [("t1", 1)])
                                kb.mm(pY[:], ident_b[:], xsd[xb][:], True, False, [("xsd", xb), "ident_b"], ["pY"])
                                for d in range(2):
                                    for hq in range(2):
                                        sub = d * 2 + hq
                                        for hh in range(4):
                                            hc = (hq * 4 + hh) * 64
                                            kb.mm(pY[:, hc:hc + 64], LTt[xb][sub][:, hh, :], xdt[xb][d][:, hc:hc + 64], False, False,
                                                  [("LTt", xb, sub), ("xdt", xb, d)], ["pY"])
                                kb.mm(pY[:], ident_b[:], t1[0][:], False, False, [("t1", 0), "ident_b"], ["pY"])
                                kb.mm(pY[:], ident_b[:], t1[1][:], False, True, [("t1", 1), "ident_b"], ["pY"])
                                kb.mm(pZ[0][:], B_tok[:, c, :], xwf[xb][:], True, True, ["B_tok", ("xwf", xb)], [("pZ", 0)])
                                kb.tt(v3(Hf[:]), v3(Hf[:]), bcast_heads(dd[:, c, 1, hs], 8, 64), ALU.mult,
                                      ["Hf", ("dd", c)], ["Hf"])
                                kb.tt(Hf[:], Hf[:], pZ[0][:], ALU.add, ["Hf", ("pZ", 0)], ["Hf"])
                                kb.act(Hfb[:], Hf[:], AF.Copy, ["Hf"], ["Hfb"])
                                kb.tt(yzb[xb][:], pY[:], szc[xb][:], ALU.mult, ["pY", ("szc", xb)], [("yzb", xb)])
                                kb.act(t1[0][:], yzb[xb][:], AF.Square, [("yzb", xb)], [("t1", 0), ("ssq4", c)],
                                       accum=ssq4[:, c, g:g + 1])

                            def stage3(c):
                                xb = c % 2
                                for bb in range(4):
                                    kb.tr(pTb[:, bb * 128:(bb + 1) * 128], yzb[xb][:, bb * 128:(bb + 1) * 128], ident_b[:],
                                          [("yzb", xb), "ident_b"], ["pTbs"])
                                kb.act(yzTs[xb][:], pTb[:, 0:512].rearrange("p (b t) -> p b t", b=4), AF.Copy,
                                       ["pTbs"], [("yzTs", xb)])
                                kb.dma(yzT_d[g * 4:(g + 1) * 4, :, c * 128:(c + 1) * 128].rearrange("b p t -> p b t"),
                                       yzTs[xb][:], [("yzTs", xb)], [("yzT_d", g, c)])

                            for i in range(NT + 2):
                                if i < NT:
                                    stage1(i)
                                if 1 <= i <= NT:
                                    stage2(i - 1)
                                    if (i - 1) in seq_last and seq_last[i - 1] < 2:
                                        store_state(Hf, "Hf", nsf, seq_last[i - 1])
                                if i >= 2:
                                    stage3(i - 2)

                    for t in range(NT):
                        kb.tt(ssq4[:, t, 0:2], ssq4[:, t, 0:2], ssq4[:, t, 2:4], ALU.add, [("ssq4", t)], [("ssq4", t)])
                        kb.tt(ssq4[:, t, 0:1], ssq4[:, t, 0:1], ssq4[:, t, 1:2], ALU.add, [("ssq4", t)], [("ssq4", t)])
                        kb.act(ssq4[:, t, 1:2], ssq4[:, t, 0:1], AF.Ln, [("ssq4", t)], [("ssq4", t)], scale=1.0 / 2048, bias=EPS)
                        kb.act(rstd_tok[:, t:t + 1], ssq4[:, t, 1:2], AF.Exp, [("ssq4", t)], [("rstd_tok", t)], scale=-0.5)

                chk(3, l)
                with contextlib.ExitStack() as st:
                    wblk = [sbt(st, f"wgb{i}", [128, 8, 128], BF16) for i in range(2)]
                    gS = [sbt(st, f"gS{i}", [128, TOK], BF16) for i in range(2)]
                    pp = [pst(st, f"ppg{i}", [128, 512], F32) for i in range(4)]
                    pi = 0
                    g0 = ph0_gen(st, 1) if l == 0 else None
                    for gi in range(16):
                        b = gi % 2
                        col0 = (GA0 if gi < 8 else GB0) + (gi % 8) * 128
                        kb.dma(wblk[b][:], wview(w_in[l], col0, 128), [], [("wgb", b)], q="pool")
                        for w in range(NW):
                            pb = pi % 4
                            pi += 1
                            for k in range(8):
                                kb.mm(pp[pb][:], wblk[b][:, k, :], hT[:, k, w * 512:(w + 1) * 512], k == 0, k == 7,
                                      [("wgb", b)] + hTw(w), [("ppg", pb)])
                            kb.act(gS[b][:, w * 512:(w + 1) * 512], pp[pb][:], AF.Sigmoid, [("ppg", pb)], [("gS", b)])
                        kb.dma(gT_d[gi], gS[b][:], [("gS", b)], [("gT_d", gi)])
                        if g0 is not None:
                            next(g0, None)
                    if g0 is not None:
                        for _ in g0:
                            pass

                chk(4, l)
                Hst.close()
                with contextlib.ExitStack() as st:
                    wna = sbt(st, "wna", [128, 8, 1024], BF16)
                    wssd = sbt(st, "wssd", [128, 16, 1024], BF16)
                    wo = sbt(st, "wo", [128, 8, 1024], BF16)
                    naw = sbt(st, "naw", [128, 8, 512], BF16)
                    yzw = sbt(st, "yzw", [128, 16, 512], BF16)
                    gaw = sbt(st, "gaw", [128, 8, 512], BF16)
                    gbw = sbt(st, "gbw", [128, 8, 512], BF16)
                    mixT2 = [sbt(st, f"mixT{i}", [128, 8, 512], BF16) for i in range(2)]
                    rbc = sbt(st, "rbc", [128, 512], F32)
                    diag = [sbt(st, f"diag{i}", [128, 128], F32) for i in range(2)]
                    u1 = [sbt(st, f"u1_{i}", [128, 512], F32) for i in range(2)]
                    u2 = [sbt(st, f"u2_{i}", [128, 512], F32) for i in range(2)]
                    xt = [sbt(st, f"xt3_{i}", [128, 1024], F32) for i in range(2)]
                    tmp = [sbt(st, f"tmp3_{i}", [128, 1024], F32) for i in range(2)]
                    x1 = [sbt(st, f"x1_{i}", [128, 1024], F32) for i in range(2)]
                    hb = [sbt(st, f"hb3_{i}", [128, 1024], BF16) for i in range(2)]
                    junk = sbt(st, "junk3", [128, 1024], BF16)
                    ss = [sbt(st, f"ss3_{i}", [128, 4], F32) for i in range(2)]
                    g1_bc = [sbt(st, f"g1_bc{r}", [128, 1024], F32) for r in range(2)]
                    s2_bc = [sbt(st, f"s2_bc{r}", [128, 1024], F32) for r in range(2)]
                    sh2_bc = [sbt(st, f"sh2_bc{r}", [128, 1024], F32) for r in range(2)]
                    g2n = sbt(st, "g2n", [128, 1024], F32)
                    h2w = sbt(st, "h2w", [128, 8, 512], BF16)
                    pA = [pst(st, f"pA{i}", [128, 512], F32) for i in range(2)]
                    pB = [pst(st, f"pB{i}", [128, 512], F32) for i in range(2)]
                    pX = [pst(st, f"pX{i}", [128, 1024], F32) for i in range(1)]
                    pR = pst(st, "pR", [128, 512], F32)
                    pT = [pst(st, f"pT3_{i}", [128, 8, 128], BF16) for i in range(1)]

                    kb.dma(wna[:], wview(w_na_out[l], 0, 1024), [], ["wna"], q="pool")
                    kb.dma(wssd[:], wview(w_ssd_out[l], 0, 1024), [], ["wssd"], q="pool")
                    kb.dma(wo[:], wview(w_o[l], 0, 1024), [], ["wo"], q="pool")
                    gn3 = sbt(st, "gn3", [128, 16], F32)
                    kb.dma(gn3[:], gn_d[l], [], ["gn3"])
                    for k in range(16):
                        kb.ts(wssd[:, k, :], wssd[:, k, :], gn3[:, k:k + 1], None, ALU.mult, None, ["wssd", "gn3"], ["wssd"])
                    kb.dma(g2n[:], norm2_g[l:l + 1, :].to_broadcast([128, 1024]), [], ["g2n"])
                    for r in range(2):
                        kb.dma(g1_bc[r][:], mod_row(l, r, 2), [("modd", l)], [("g1_bc", r)])
                        kb.dma(s2_bc[r][:], mod_row(l, r, 4), [("modd", l)], [("s2_bc", r)])
                        kb.dma(sh2_bc[r][:], mod_row(l, r, 3), [("modd", l)], [("sh2_bc", r)])
                        kb.stt(s2_bc[r][:], s2_bc[r][:], 1.0, g2n[:], ALU.add, ALU.mult, [("s2_bc", r), "g2n"], [("s2_bc", r)])
                    ai = 0
                    def ph3_loads(w):
                        ws = slice(w * 512, (w + 1) * 512)
                        kb.dma(naw[:], naT_d[:, :, ws].rearrange("b p t -> p b t"), [("naT_d", hp) for hp in range(8)], ["naw"])
                        kb.dma(yzw[:], yzT_d[:, :, ws].rearrange("b p t -> p b t"),
                               [("yzT_d", g, c) for g in range(4) for c in range(w * 4, w * 4 + 4)], ["yzw"])
                        kb.dma(gaw[:], gT_d[0:8, :, ws].rearrange("b p t -> p b t"), [("gT_d", gi) for gi in range(8)], ["gaw"])
                        kb.dma(gbw[:], gT_d[8:16, :, ws].rearrange("b p t -> p b t"), [("gT_d", gi) for gi in range(8, 16)], ["gbw"])

                    def rstd_part(w):
                        for ti in range(4):
                            t = w * 4 + ti
                            db = ti % 2
                            kb.ts(diag[db][:], ident_f[:], rstd_tok[:, t:t + 1], None, ALU.mult, None,
                                  ["ident_f", ("rstd_tok", t)], [("diag", db)])
                            kb.mm(pR[:, ti * 128:(ti + 1) * 128], ones_f[:], diag[db][:], True, True,
                                  ["ones_f", ("diag", db)], ["pR"])
                        kb.act(rbc[:], pR[:], AF.Copy, ["pR"], ["rbc"])

                    def mix_part(w, cbk):
                        nonlocal ai
                        mixT = mixT2[w % 2]
                        ab = ai % 2
                        ai += 1
                        cs = slice(cbk * 128, (cbk + 1) * 128)
                        for k in range(8):
                            kb.mm(pA[ab][:], wna[:, k, cs], naw[:, k, :], k == 0, k == 7, ["wna", "naw"], [("pA", ab)])
                        for k in range(16):
                            kb.mm(pB[ab][:], wssd[:, k, cs], yzw[:, k, :], k == 0, k == 15, ["wssd", "yzw"], [("pB", ab)])
                        kb.tt(u1[ab][:], pA[ab][:], gaw[:, cbk, :], ALU.mult, [("pA", ab), "gaw"], [("u1", ab)])
                        kb.tt(u2[ab][:], pB[ab][:], rbc[:], ALU.mult, [("pB", ab), "rbc"], [("u2", ab)])
                        kb.tt(u2[ab][:], u2[ab][:], gbw[:, cbk, :], ALU.mult, [("u2", ab), "gbw"], [("u2", ab)])
                        kb.tt(mixT[:, cbk, :], u1[ab][:], u2[ab][:], ALU.add, [("u1", ab), ("u2", ab)], [("mixT", w % 2)])

                    def woA(w, ti):
                        mixT = mixT2[w % 2]
                        t = w * 4 + ti
                        b = t % 2
                        r = 0 if t < 4 else 1
                        for half in range(2):
                            for k in range(8):
                                kb.mm(pX[0][:, half * 512:(half + 1) * 512], mixT[:, k, ti * 128:(ti + 1) * 128],
                                      wo[:, k, half * 512:(half + 1) * 512], k == 0, k == 7, [("mixT", w % 2), "wo"], ["pX"])
                        kb.dma(xt[b][:], xsrc[t * 128:(t + 1) * 128, :], [(xskey, t)], [("xt3", b)])
                        kb.tt(tmp[b][:], pX[0][:], g1_bc[r][:], ALU.mult, ["pX", ("g1_bc", r)], [("tmp3", b)])
                        kb.tt(x1[b][:], tmp[b][:], xt[b][:], ALU.add, [("tmp3", b), ("xt3", b)], [("x1", b)], eng="pool")
                        kb.dma(x1d[t * 128:(t + 1) * 128, :], x1[b][:], [("x1", b)], [("x1d", t)])
                        kb.act(junk[:], x1[b][:], AF.Square, [("x1", b)], [("njunk",), ("nss", b)], accum=ss[b][:, 0:1])
                        kb.act(ss[b][:, 1:2], ss[b][:, 0:1], AF.Ln, [("nss", b)], [("nss", b)], scale=1.0 / 1024, bias=EPS)
                        kb.act(ss[b][:, 2:3], ss[b][:, 1:2], AF.Exp, [("nss", b)], [("nss", b)], scale=-0.5)
                        kb.stt(tmp[b][:], x1[b][:], ss[b][:, 2:3], s2_bc[r][:], ALU.mult, ALU.mult,
                               [("x1", b), ("nss", b), ("s2_bc", r)], [("tmp3", b)])
                        kb.tt(hb[b][:], tmp[b][:], sh2_bc[r][:], ALU.add, [("tmp3", b), ("sh2_bc", r)], [("nhb", b)])

                    def woB(w, ti):
                        t = w * 4 + ti
                        b = t % 2
                        for c in range(8):
                            kb.tr(pT[0][:, c, :], hb[b][:, c * 128:(c + 1) * 128], ident_b[:], [("nhb", b), "ident_b"], [("npT", 0)])
                        kb.act(h2w[:, :, ti * 128:(ti + 1) * 128], pT[0][:], AF.Copy, [("npT", 0)], ["h2w"])
                        if ti == 3:
                            kb.dma(h2T_d[:, :, w * 512:(w + 1) * 512].rearrange("b p t -> p b t"), h2w[:], ["h2w"], [("h2T_d", w)])

                    ph3_loads(0)
                    rstd_part(0)
                    for cbk in range(8):
                        mix_part(0, cbk)
                    if NW > 1:
                        ph3_loads(1)
                    for w in range(NW):
                        if w + 1 < NW:
                            rstd_part(w + 1)
                        for slot in range(8):
                            if w + 1 < NW:
                                mix_part(w + 1, slot)
                            if slot % 2 == 0:
                                woA(w, slot // 2)
                            else:
                                woB(w, slot // 2)
                        if w + 2 < NW:
                            ph3_loads(w + 2)

            chk(5, l)
            with contextlib.ExitStack() as Fst:
                actT = sbt(Fst, "actT", [128, 22, TOK], BF16)
                with contextlib.ExitStack() as st:
                    h2T = sbt(st, "h2T", [128, 8, TOK], BF16)
                    wvg = [sbt(st, f"wvg{i}", [128, 8, 128], BF16) for i in range(2)] * 2
                    UcpF2 = [sbt(st, f"UcpF{i}", [128, UCW], F32) for i in range(2)]
                    cval = sbt(st, "cval", [128, UCW], F32)
                    cgate = sbt(st, "cgate", [128, UCW], F32)
                    fwt = sbt(st, "fwt", [128, 44, 3], F32)
                    fbt = sbt(st, "fbt", [128, 44], F32)
                    pp = [pst(st, f"ppf{i}", [128, 512], F32) for i in range(4)]
                    for i in range(2):
                        kb.memset(UcpF2[i][:], 0.0, [("UcpF", i)])
                    kb.dma(fwt[:], fw_d[l], [], ["fwt"])
                    kb.dma(fbt[:], fb_d[l], [], ["fbt"])
                    for w in range(NW):
                        kb.dma(h2T[:, :, w * 512:(w + 1) * 512], h2T_d[:, :, w * 512:(w + 1) * 512].rearrange("b p t -> p b t"),
                               [("h2T_d", w)], [("h2T", w)])
                    pi = 0
                    wi = 0
                    def ffA1(j, half):
                        nonlocal wi, pi
                        wb_ = wi % 2
                        ub = wi % 2
                        Ucp = UcpF2[ub]
                        wi += 1
                        kb.dma(wvg[wb_][:], wview(w_up[l], half * DFF + j * 128, 128), [], [("wvg", wb_)], q="pool")
                        for w in range(NW):
                            pb = pi % 4
                            pi += 1
                            for k in range(8):
                                kb.mm(pp[pb][:], wvg[wb_][:, k, :], h2T[:, k, w * 512:(w + 1) * 512], k == 0, k == 7,
                                      [("wvg", wb_), ("h2T", w)], [("ppf", pb)])
                            if w == 0:
                                kb.act(Ucp[:, 2:258], pp[pb][:, 0:256], AF.Copy, [("ppf", pb)], [("UcpF", ub)])
                                kb.act(Ucp[:, 260:516], pp[pb][:, 256:512], AF.Copy, [("ppf", pb)], [("UcpF", ub)])
                            else:
                                c0 = 518 + (w - 1) * 512
                                kb.act(Ucp[:, c0:c0 + 512], pp[pb][:], AF.Copy, [("ppf", pb)], [("UcpF", ub)])
                        return ub

                    def ffA2(j, half, ub):
                        Ucp = UcpF2[ub]
                        blk = half * 22 + j
                        dst, dk = (cval, "cval") if half == 0 else (cgate, "cgate")
                        kb.ts(dst[:, 2:2566], Ucp[:, 2:2566], fwt[:, blk, 1:2], fbt[:, blk:blk + 1], ALU.mult, ALU.add,
                              [("UcpF", ub), "fwt", "fbt"], [dk])
                        for (kk, sh) in ((0, -1), (2, 1)):
                            kb.stt(dst[:, 2:2566], Ucp[:, 2 + sh:2566 + sh], fwt[:, blk, kk:kk + 1], dst[:, 2:2566],
                                   ALU.mult, ALU.add, [("UcpF", ub), "fwt", dk], [dk])

                    def ffS(j):
                        kb.act(cgate[:, 2:2566], cgate[:, 2:2566], AF.Silu, ["cgate"], ["cgate"])
                        for (a, z, c0) in SEGCOL:
                            kb.tt(actT[:, j, a:z], cval[:, c0:c0 + (z - a)], cgate[:, c0:c0 + (z - a)], ALU.mult,
                                  ["cval", "cgate"], [("actT", j)])

                    for j in range(23):
                        if j < 22:
                            ub0 = ffA1(j, 0)
                        if j >= 1:
                            ffS(j - 1)
                        if j < 22:
                            ffA2(j, 0, ub0)
                            ub1 = ffA1(j, 1)
                            ffA2(j, 1, ub1)

                chk(6, l)
                with contextlib.ExitStack() as st:
                    wdn = sbt(st, "wdn", [128, 22, 1024], BF16)
                    g2_bc = [sbt(st, f"g2_bc{r}", [128, 1024], F32) for r in range(2)]
                    xt = [sbt(st, f"xt5_{i}", [128, 1024], F32) for i in range(2)]
                    tmp = [sbt(st, f"tmp5_{i}", [128, 1024], F32) for i in range(2)]
                    x2 = [sbt(st, f"x2_{i}", [128, 1024], F32) for i in range(2)]
                    pX = [pst(st, f"pX5_{i}", [128, 1024], F32) for i in range(2)]
                    for kq in range(2):
                        kb.dma(wdn[:, kq * 11:(kq + 1) * 11, :],
                               w_down[l].rearrange("(k p) c -> p k c", p=128)[:, kq * 11:(kq + 1) * 11, :], [], [("wdn", kq)],
                               q="pool")
                    for r in range(2):
                        kb.dma(g2_bc[r][:], mod_row(l, r, 5), [("modd", l)], [("g2_bc", r)])
                    actk = [("actT", j) for j in range(22)]
                    for t in range(NT):
                        b = t % 2
                        r = 0 if t < 4 else 1
                        for half in range(2):
                            for k in range(22):
                                kb.mm(pX[b][:, half * 512:(half + 1) * 512], actT[:, k, t * 128:(t + 1) * 128],
                                      wdn[:, k, half * 512:(half + 1) * 512], k == 0, k == 21,
                                      actk + [("wdn", 0), ("wdn", 1)], [("pX5", b)])
                        kb.dma(xt[b][:], x1d[t * 128:(t + 1) * 128, :], [("x1d", t)], [("xt5", b)])
                        kb.tt(tmp[b][:], pX[b][:], g2_bc[r][:], ALU.mult, [("pX5", b), ("g2_bc", r)], [("tmp5", b)])
                        kb.tt(x2[b][:], tmp[b][:], xt[b][:], ALU.add, [("tmp5", b), ("xt5", b)], [("x2", b)])
                        kb.dma(xfin[t * 128:(t + 1) * 128, :], x2[b][:], [("x2", b)], [(xfkey, t)])

            chk(7, l)
        P.emit(nc)
    return kb


def _consts():
    c = {}
    c["c_ident"] = np.eye(128, dtype=np.float32)
    t = np.arange(128)[:, None]
    j = np.arange(128)[None, :]
    c["c_tri"] = np.stack([(t > j), (t < j), (t <= j), (t >= j)], axis=1).astype(np.float32)
    bd = np.zeros((128, 128), np.float32)
    bd[:64, :64] = 1.0
    bd[64:, 64:] = 1.0
    c["c_bd"] = bd
    oz = np.zeros((128, 2, 128), np.float32)
    oz[:, 0, :64] = 1.0
    oz[:, 1, 64:] = 1.0
    c["c_onesz"] = oz
    cols = np.arange(64)
    cs = np.clip(cols - 8, 0, 48)
    cm = (cols[None, :] >= cs[:, None]) & (cols[None, :] < cs[:, None] + 16)
    cmT = cm.T.astype(np.float32)
    m = np.zeros((128, 2, 22, 64), np.float32)
    for par in range(2):
        for i in range(22):
            dr = 10 + par - i
            if -7 <= dr <= 7:
                m[par * 64:(par + 1) * 64, 0, i, :] = cmT
            if -4 <= dr <= 3:
                m[par * 64:(par + 1) * 64, 1, i, :] = cmT
    c["c_mask"] = m
    return c


_CACHE = {}
NCORES_RUN = 8


def kernel(x_prompt, x_sample, c, cache_k, cache_v, state_ssd_fwd, state_ssd_bwd, c_ctx,
           w_ada, b_ada, norm1_g, w_in, q_norm_g, k_norm_g, rpb, ssd_conv_w, ssd_conv_b,
           a_log, dt_bias, d_skip, ssd_norm_g, w_na_out, w_ssd_out, w_o, norm2_g,
           w_up, ffn_conv_w, ffn_conv_b, w_down):
    f = lambda a: np.ascontiguousarray(np.asarray(a, dtype=np.float32))
    x_prompt, x_sample, c, cache_k, cache_v = f(x_prompt), f(x_sample), f(c), f(cache_k), f(cache_v)
    state_ssd_fwd, state_ssd_bwd, c_ctx = f(state_ssd_fwd), f(state_ssd_bwd), f(c_ctx)
    if "kb" not in _CACHE:
        _CACHE["kb"] = build()
    kb = _CACHE["kb"]

    shared = dict(_consts())
    shared["w_ada"] = f(w_ada)
    shared["b_ada"] = f(b_ada)
    shared["norm1_g"] = f(norm1_g)
    shared["w_in"] = f(w_in)
    qg, kg = f(q_norm_g), f(k_norm_g)
    gqk = np.zeros((128, 4), np.float32)
    for l in range(DEPTH):
        gqk[:, 2 * l + 0] = np.tile(qg[l], 2)
        gqk[:, 2 * l + 1] = np.tile(kg[l], 2)
    shared["gqk"] = gqk
    kc = np.arange(64)[:, None]
    qc = np.arange(64)[None, :]
    cidx = np.clip(kc - qc + 15, 0, 30)
    rp = f(rpb)[:, :, ::-1, :]
    G = rp[:, :, :, cidx]
    shared["rpbG"] = np.ascontiguousarray(np.transpose(G, (0, 1, 3, 2, 4)))
    shared["cw"] = np.ascontiguousarray(f(ssd_conv_w).reshape(DEPTH, 4, 24, 128).transpose(0, 3, 2, 1))
    shared["cb"] = np.ascontiguousarray(f(ssd_conv_b).reshape(DEPTH, 24, 128).transpose(0, 2, 1))
    shared["a_log"] = f(a_log).reshape(DEPTH, 64)
    shared["dt_bias"] = f(dt_bias).reshape(DEPTH, 64)
    shared["d_skip"] = f(d_skip)
    shared["gn"] = np.ascontiguousarray(f(ssd_norm_g).reshape(DEPTH, 16, 128).transpose(0, 2, 1))
    shared["w_na_out"] = f(w_na_out)
    shared["w_ssd_out"] = f(w_ssd_out)
    shared["w_o"] = f(w_o)
    shared["norm2_g"] = f(norm2_g)
    shared["w_up"] = f(w_up)
    shared["fw"] = np.ascontiguousarray(f(ffn_conv_w).reshape(DEPTH, 3, 44, 128).transpose(0, 3, 2, 1))
    shared["fb"] = np.ascontiguousarray(f(ffn_conv_b).reshape(DEPTH, 44, 128).transpose(0, 2, 1))
    shared["w_down"] = f(w_down)

    in_maps = []
    for i in range(NCORES_RUN):
        m = dict(shared)
        m["xin"] = np.concatenate([x_prompt[2 * i].reshape(256, 1024), x_prompt[2 * i + 1].reshape(256, 1024),
                                   x_sample[i].reshape(2048, 1024)], axis=0)
        cvec = np.stack([c_ctx, c[i]], axis=0)
        m["cvecT"] = np.ascontiguousarray(cvec.reshape(2, 8, 128).transpose(2, 1, 0))
        m["ck"] = np.ascontiguousarray(cache_k[i].reshape(DEPTH, 256, 1024))
        m["cv"] = np.ascontiguousarray(cache_v[i].reshape(DEPTH, 256, 1024))
        m["stf"] = np.ascontiguousarray(state_ssd_fwd[i].reshape(DEPTH, 2048, 128))
        m["stb"] = np.ascontiguousarray(state_ssd_bwd[i].reshape(DEPTH, 2048, 128))
        for k_, shp in kb.ins.items():
            assert tuple(m[k_].shape) == tuple(shp), (k_, m[k_].shape, shp)
        in_maps.append(m)

    if NCORES_RUN != 8:
        return run_bass_kernel_spmd(kb.nc, in_maps, core_ids=list(range(NCORES_RUN))).results
    res = run_bass_kernel_spmd(kb.nc, in_maps, core_ids=list(range(8)))
    R = res.results
    y_prompt = np.zeros((16, 256, 1024), np.float32)
    y_sample = np.zeros((8, 2048, 1024), np.float32)
    nck = np.zeros((16, DEPTH, 256, 16, 64), np.float32)
    ncv = np.zeros((16, DEPTH, 256, 16, 64), np.float32)
    nsf = np.zeros((16, DEPTH, 32, 64, 128), np.float32)
    nsb = np.zeros((16, DEPTH, 32, 64, 128), np.float32)
    for i in range(8):
        yo = np.asarray(R[i]["yout"])
        y_prompt[2 * i] = yo[0:256]
        y_prompt[2 * i + 1] = yo[256:512]
        y_sample[i] = yo[512:]
        for s in range(2):
            nck[2 * i + s] = np.asarray(R[i]["nk"])[s].reshape(DEPTH, 256, 16, 64)
            ncv[2 * i + s] = np.asarray(R[i]["nv"])[s].reshape(DEPTH, 256, 16, 64)
            nsf[2 * i + s] = np.asarray(R[i]["nsf"])[s].reshape(DEPTH, 32, 64, 128)
            nsb[2 * i + s] = np.asarray(R[i]["nsb"])[s].reshape(DEPTH, 32, 64, 128)
    return (y_prompt, y_sample, nck, ncv, nsf, nsb)
```

```python
import contextlib
import numpy as np
import concourse.bass as bass
import concourse.mybir as mybir
from concourse.bass_utils import run_bass_kernel_spmd

F32 = mybir.dt.float32
BF16 = mybir.dt.bfloat16
AF = mybir.ActivationFunctionType
ALU = mybir.AluOpType

EPOCH = 30000
NDMASEM = 12

DEPTH = 2
NT = 20
NW = 5
TOK = 2560
Q0, K0, V0, Z0, X0, B0, C0, DT0, GA0, GB0 = 0, 1024, 2048, 3072, 5120, 7168, 7680, 8192, 8256, 9280
PIN = 10304
DFF = 2816
SEQS = [(0, [0, 1]), (1, [2, 3]), (2, list(range(4, 20)))]
EPS = 1e-6
SEGCOL = [(0, 256, 2), (256, 512, 260), (512, 2560, 518)]
UCW = 2568


class Op:
    __slots__ = ("eng", "fn", "reads", "writes", "dma", "deps", "sig", "idx", "dsem", "dval", "n")

    def __init__(self, eng, fn, reads, writes, dma):
        self.eng = eng
        self.fn = fn
        self.reads = reads
        self.writes = writes
        self.dma = dma
        self.deps = []
        self.sig = False
        self.dsem = None
        self.dval = 0
        self.n = 0


class Prog:
    ENGS = ("pe", "act", "dve", "pool", "sp")

    def __init__(self):
        self.ops = []
        self.fences = []

    def fence(self):
        if not self.fences or self.fences[-1] != len(self.ops):
            self.fences.append(len(self.ops))

    def add(self, eng, fn, reads=(), writes=(), dma=False):
        op = Op(eng, fn, tuple(reads), tuple(writes), dma)
        op.idx = len(self.ops)
        self.ops.append(op)
        return op

    def schedule(self):
        ops = self.ops
        last_w = {}
        readers = {}
        fences = list(self.fences)
        fi = 0
        fence_deps = []
        seen_fence = {e: 0 for e in self.ENGS}
        last_comp = {}
        pend_dma = []
        for op in ops:
            while fi < len(fences) and fences[fi] <= op.idx:
                fence_deps = list(last_comp.values()) + pend_dma
                pend_dma = []
                fi += 1
            deps = set()
            for r in op.reads:
                w = last_w.get(r)
                if w is not None:
                    deps.add(w)
            for w_ in op.writes:
                w = last_w.get(w_)
                if w is not None:
                    deps.add(w)
                for rd in readers.get(w_, ()):
                    deps.add(rd)
            for r in op.reads:
                readers.setdefault(r, []).append(op.idx)
            for w_ in op.writes:
                last_w[w_] = op.idx
                readers[w_] = []
            deps.discard(op.idx)
            keep = []
            best = {}
            for d in deps:
                p = ops[d]
                if p.dma:
                    keep.append(d)
                    continue
                if p.eng == op.eng and not op.dma and op.eng == "pe":
                    raw = any(r in p.writes for r in op.reads)
                    if not raw:
                        continue
                b = best.get(p.eng)
                if b is None or d > b:
                    best[p.eng] = d
            keep.extend(best.values())
            if seen_fence[op.eng] < fi:
                seen_fence[op.eng] = fi
                for d in fence_deps:
                    p = ops[d]
                    if (not p.dma) and p.eng == op.eng and not op.dma:
                        continue
                    keep.append(d)
            if op.dma:
                pend_dma.append(op.idx)
            else:
                last_comp[op.eng] = op.idx
            op.deps = keep
            for d in keep:
                ops[d].sig = True
        cnt = {e: 0 for e in self.ENGS}
        dcnt = {}
        dprev = {}
        dptr = {e: 0 for e in self.ENGS}
        for op in ops:
            if op.dma:
                s = (op.eng, dptr[op.eng] % NDMASEM)
                dptr[op.eng] += 1
                prev = dprev.get(s)
                if prev is not None:
                    op.deps.append(prev)
                dprev[s] = op.idx
                dcnt[s] = dcnt.get(s, 0) + 16
                op.dsem = s
                op.dval = dcnt[s]
            elif op.sig:
                cnt[op.eng] += 1
                op.n = cnt[op.eng]
        self.cnt = cnt
        self.dcnt = dcnt

    def emit(self, nc):
        self.schedule()
        ops = self.ops
        with contextlib.ExitStack() as st:
            esems = {}
            for e in self.ENGS:
                k = (self.cnt[e] + EPOCH - 1) // EPOCH
                esems[e] = [st.enter_context(nc.semaphore(f"s_{e}_{i}")) for i in range(max(k, 1))]
            dsems = {}
            for s in self.dcnt:
                dsems[s] = st.enter_context(nc.semaphore(f"d_{s[0]}_{s[1]}"))
            block = st.enter_context(nc.Block())
            per_eng = {e: [o for o in ops if o.eng == e] for e in self.ENGS}

            def make(e):
                def body(eng):
                    known = {x: 0 for x in self.ENGS}
                    dknown = {}
                    for op in per_eng[e]:
                        needc = {}
                        needd = {}
                        for d in op.deps:
                            p = ops[d]
                            if p.dma:
                                if p.dval > needd.get(p.dsem, 0):
                                    needd[p.dsem] = p.dval
                            elif p.n > needc.get(p.eng, 0):
                                needc[p.eng] = p.n
                        for ds, dv in needd.items():
                            if dknown.get(ds, 0) >= dv:
                                continue
                            eng.wait_ge(dsems[ds], dv)
                            dknown[ds] = dv
                        for pe_, n_ in needc.items():
                            if known[pe_] >= n_:
                                continue
                            eng.wait_ge(esems[pe_][(n_ - 1) // EPOCH], (n_ - 1) % EPOCH + 1)
                            known[pe_] = n_
                        ins = op.fn(eng)
                        if op.dma:
                            ins.then_inc(dsems[op.dsem], 16)
                        elif op.sig:
                            ins.then_inc(esems[e][(op.n - 1) // EPOCH], 1)
                    if e == "sp":
                        for s, v in self.dcnt.items():
                            if dknown.get(s, 0) < v:
                                eng.wait_ge(dsems[s], v)
                return body

            block.tensor(make("pe"))
            block.scalar(make("act"))
            block.vector(make("dve"))
            block.gpsimd(make("pool"))
            block.sync(make("sp"))


class KB:
    def __init__(self):
        self.nc = bass.Bass("TRN2", target_bir_lowering=False)
        self.P = Prog()
        self.ins = {}
        self.n_uid = 0

    def din(self, name, shape):
        ap = self.nc.dram_tensor(name, list(shape), F32, kind="ExternalInput").ap()
        self.ins[name] = tuple(shape)
        return ap

    def dout(self, name, shape):
        return self.nc.dram_tensor(name, list(shape), F32, kind="ExternalOutput").ap()

    def dscr(self, name, shape, dt):
        if DBG:
            return self.nc.dram_tensor(name, list(shape), dt, kind="ExternalOutput").ap()
        return self.nc.dram_tensor(name, list(shape), dt).ap()

    def mm(self, out, lhsT, rhs, start, stop, R, W):
        self.P.add("pe", lambda e: e.matmul(out, lhsT=lhsT, rhs=rhs, start=start, stop=stop), R, W)

    def tr(self, out, in_, ident, R, W):
        self.P.add("pe", lambda e: e.transpose(out=out, in_=in_, identity=ident), R, W)

    def act(self, out, in_, func, R, W, scale=None, bias=None, accum=None):
        kw = {}
        if scale is not None:
            kw["scale"] = scale
        if bias is not None:
            kw["bias"] = bias
        if accum is not None:
            kw["accum_out"] = accum
        self.P.add("act", lambda e: e.activation(out=out, in_=in_, func=func, **kw), R, W)

    def tt(self, out, in0, in1, op, R, W, eng="dve"):
        self.P.add(eng, lambda e: e.tensor_tensor(out=out, in0=in0, in1=in1, op=op), R, W)

    def ts(self, out, in0, s1, s2, op0, op1, R, W, eng="dve"):
        if s2 is None:
            self.P.add(eng, lambda e: e.tensor_scalar(out=out, in0=in0, scalar1=s1, scalar2=None, op0=op0), R, W)
        else:
            self.P.add(eng, lambda e: e.tensor_scalar(out=out, in0=in0, scalar1=s1, scalar2=s2, op0=op0, op1=op1), R, W)

    def stt(self, out, in0, scalar, in1, op0, op1, R, W, eng="dve"):
        self.P.add(eng, lambda e: e.scalar_tensor_tensor(out=out, in0=in0, scalar=scalar, in1=in1, op0=op0, op1=op1), R, W)

    def cp(self, out, in_, R, W, eng="dve"):
        self.P.add(eng, lambda e: e.tensor_copy(out=out, in_=in_), R, W)

    def recip(self, out, in_, R, W):
        self.P.add("dve", lambda e: e.reciprocal(out=out, in_=in_), R, W)

    def memset(self, ap, val, W, eng="dve"):
        self.P.add(eng, lambda e: e.memset(ap, val), (), W)

    def dma(self, out, in_, R, W, q="sp"):
        self.P.add(q, lambda e: e.dma_start(out=out, in_=in_), R, W, dma=True)

    def uid(self, s):
        self.n_uid += 1
        return (s, self.n_uid)


def bcast_heads(ap2, nh, p):
    return ap2.unsqueeze(2).to_broadcast([128, nh, p])


STOP = None
DBG = False


class _Stop(Exception):
    pass


def build():
    kb = KB()
    try:
        _build_inner(kb)
    except _Stop:
        pass
    return kb


def _build_inner(kb):
    nc = kb.nc
    P = kb.P
    APc = None

    xin = kb.din("xin", [TOK, 1024])
    cvecT = kb.din("cvecT", [128, 8, 2])
    ck = kb.din("ck", [DEPTH, 256, 1024])
    cv = kb.din("cv", [DEPTH, 256, 1024])
    stf = kb.din("stf", [DEPTH, 2048, 128])
    stb = kb.din("stb", [DEPTH, 2048, 128])
    w_ada = kb.din("w_ada", [DEPTH, 1024, 6144])
    b_ada = kb.din("b_ada", [DEPTH, 6144])
    norm1_g = kb.din("norm1_g", [DEPTH, 1024])
    w_in = kb.din("w_in", [DEPTH, 1024, PIN])
    gqk_d = kb.din("gqk", [128, 4])
    rpbG = kb.din("rpbG", [DEPTH, 16, 64, 15, 64])
    cw_d = kb.din("cw", [DEPTH, 128, 24, 4])
    cb_d = kb.din("cb", [DEPTH, 128, 24])
    a_log = kb.din("a_log", [DEPTH, 64])
    dt_bias = kb.din("dt_bias", [DEPTH, 64])
    d_skip = kb.din("d_skip", [DEPTH, 32])
    gn_d = kb.din("gn", [DEPTH, 128, 16])
    w_na_out = kb.din("w_na_out", [DEPTH, 1024, 1024])
    w_ssd_out = kb.din("w_ssd_out", [DEPTH, 2048, 1024])
    w_o = kb.din("w_o", [DEPTH, 1024, 1024])
    norm2_g = kb.din("norm2_g", [DEPTH, 1024])
    w_up = kb.din("w_up", [DEPTH, 1024, 2 * DFF])
    fw_d = kb.din("fw", [DEPTH, 128, 44, 3])
    fb_d = kb.din("fb", [DEPTH, 128, 44])
    w_down = kb.din("w_down", [DEPTH, DFF, 1024])
    c_ident = kb.din("c_ident", [128, 128])
    c_tri = kb.din("c_tri", [128, 4, 128])
    c_bd = kb.din("c_bd", [128, 128])
    c_onesz = kb.din("c_onesz", [128, 2, 128])
    c_mask = kb.din("c_mask", [128, 2, 22, 64])

    yout = kb.dout("yout", [TOK, 1024])
    nk = kb.dout("nk", [2, DEPTH, 256, 1024])
    nv = kb.dout("nv", [2, DEPTH, 256, 1024])
    nsf = kb.dout("nsf", [2, DEPTH, 2048, 128])
    nsb = kb.dout("nsb", [2, DEPTH, 2048, 128])

    modd = kb.dscr("modd", [DEPTH, 2, 6144], F32)
    x1d = kb.dscr("x1d", [TOK, 1024], F32)
    x2d = kb.dscr("x2d", [TOK, 1024], F32)
    naT_d = kb.dscr("naT_d", [8, 128, TOK], BF16)
    yzT_d = kb.dscr("yzT_d", [16, 128, TOK], BF16)
    gT_d = kb.dscr("gT_d", [16, 128, TOK], BF16)
    h2T_d = kb.dscr("h2T_d", [8, 128, TOK], BF16)

    def chk(n, l=0):
        if float(n) == int(n):
            P.fence()
        if STOP == n and l == 0:
            P.emit(nc)
            raise _Stop()

    def wview(w2d, c0, n):
        return w2d.rearrange("(k p) c -> p k c", p=128)[:, :, c0:c0 + n]

    with contextlib.ExitStack() as top:
        def sbt(st, name, shape, dt):
            kb.n_uid += 1
            return st.enter_context(nc.sbuf_tensor(f"{name}_s{kb.n_uid}", list(shape), dt))

        def pst(st, name, shape, dt):
            kb.n_uid += 1
            return st.enter_context(nc.psum_tensor(f"{name}_p{kb.n_uid}", list(shape), dt))

        ident_f = sbt(top, "ident_f", [128, 128], F32)
        ident_b = sbt(top, "ident_b", [128, 128], BF16)
        tri_f = sbt(top, "tri_f", [128, 4, 128], F32)
        tri_b = sbt(top, "tri_b", [128, 4, 128], BF16)
        ones_f = sbt(top, "ones_f", [128, 128], F32)
        bd_b = sbt(top, "bd_b", [128, 128], BF16)
        onesz_b = sbt(top, "onesz_b", [128, 2, 128], BF16)
        scT = sbt(top, "scT", [128, 8, 2], BF16)
        gqk = sbt(top, "gqk", [128, 4], F32)
        APc = type(ident_f[:])

        kb.dma(ident_f[:], c_ident, [], ["ident_f"])
        kb.dma(tri_f[:], c_tri, [], ["tri_f"])
        kb.dma(ident_b[:], c_ident, [], ["ident_b"], q="pool")
        kb.dma(tri_b[:], c_tri, [], ["tri_b"], q="pool")
        kb.dma(bd_b[:], c_bd, [], ["bd_b"], q="pool")
        kb.dma(onesz_b[:], c_onesz, [], ["onesz_b"], q="pool")
        kb.dma(gqk[:], gqk_d, [], ["gqk"])
        kb.memset(ones_f[:], 1.0, ["ones_f"])
        kb.ts(gqk[:, 0:1], gqk[:, 0:1], 0.125, None, ALU.mult, None, ["gqk"], ["gqk"])
        kb.ts(gqk[:, 2:3], gqk[:, 2:3], 0.125, None, ALU.mult, None, ["gqk"], ["gqk"])

        def ph0_gen(st, l):
            wada = [sbt(st, f"wada{i}", [128, 8, 512], BF16) for i in range(2)]
            bad = [sbt(st, f"bad{i}", [2, 512], F32) for i in range(2)]
            mst = [sbt(st, f"mst{i}", [2, 512], F32) for i in range(2)]
            pm = [pst(st, f"pm{i}", [128, 512], F32) for i in range(2)]
            for ct in range(12):
                b = ct % 2
                kb.dma(wada[b][:], wview(w_ada[l], ct * 512, 512), [], [("wada", b)], q="pool")
                kb.dma(bad[b][:], b_ada[l:l + 1, ct * 512:(ct + 1) * 512].to_broadcast([2, 512]), [], [("bad", b)])
                for k in range(8):
                    kb.mm(pm[b][0:2, :], scT[:, k, :], wada[b][:, k, :], k == 0, k == 7,
                          ["scT", ("wada", b)], [("pm", b)])
                kb.tt(mst[b][:], pm[b][0:2, :], bad[b][:], ALU.add, [("pm", b), ("bad", b)], [("mst", b)])
                kb.dma(modd[l, :, ct * 512:(ct + 1) * 512], mst[b][:], [("mst", b)], [("modd", l)])
                yield

        with contextlib.ExitStack() as st:
            csT = sbt(st, "csT", [128, 8, 2], F32)
            kb.dma(csT[:], cvecT, [], ["csT"])
            kb.act(scT[:], csT[:], AF.Silu, ["csT"], ["scT"])
            for _ in ph0_gen(st, 0):
                pass

        chk(0)

        def mod_row(l, r, idx):
            return modd[l, r:r + 1, idx * 1024:(idx + 1) * 1024].to_broadcast([128, 1024])

        def norm_tile(xt, xkey, s_bc, sh_bc, bckeys, junk, ss, tmp, hb, pT, dest, destkey, b, tmpkey=None):
            tmpkey = tmpkey or ("ntmp", b)
            kb.act(junk[:], xt, AF.Square, [xkey], [("njunk",), ("nss", b)], accum=ss[:, 0:1])
            kb.act(ss[:, 1:2], ss[:, 0:1], AF.Ln, [("nss", b)], [("nss", b)], scale=1.0 / 1024, bias=EPS)
            kb.act(ss[:, 2:3], ss[:, 1:2], AF.Exp, [("nss", b)], [("nss", b)], scale=-0.5)
            kb.stt(tmp[:], xt, ss[:, 2:3], s_bc[:], ALU.mult, ALU.mult, [xkey, ("nss", b)] + bckeys, [tmpkey])
            kb.tt(hb[:], tmp[:], sh_bc[:], ALU.add, [tmpkey] + bckeys, [("nhb", b)])
            for c in range(8):
                kb.tr(pT[:, c, :], hb[:, c * 128:(c + 1) * 128], ident_b[:], [("nhb", b), "ident_b"], [("npT", b)])
            kb.act(dest, pT[:], AF.Copy, [("npT", b)], [destkey])

        for l in range(DEPTH):
            xsrc = xin if l == 0 else x2d
            xfin = x2d if l == 0 else yout
            xskey = "x2d" if l == 1 else "xin"
            xfkey = "x2d" if l == 0 else "yout"
            with contextlib.ExitStack() as Lst:
                rstd_tok = sbt(Lst, "rstd_tok", [128, NT], F32)
                Hst = contextlib.ExitStack()
                hT = sbt(Hst, "hT", [128, 8, TOK], BF16)
                with contextlib.ExitStack() as st:
                    s_bc = [sbt(st, f"s_bc{r}", [128, 1024], F32) for r in range(2)]
                    sh_bc = [sbt(st, f"sh_bc{r}", [128, 1024], F32) for r in range(2)]
                    g1 = sbt(st, "g1", [128, 1024], F32)
                    xt = [sbt(st, f"xt{i}", [128, 1024], F32) for i in range(2)]
                    junk = sbt(st, "junk", [128, 1024], BF16)
                    tmp = [sbt(st, f"tmp{i}", [128, 1024], F32) for i in range(2)]
                    hb = [sbt(st, f"hb{i}", [128, 1024], BF16) for i in range(2)]
                    ss = [sbt(st, f"ss{i}", [128, 4], F32) for i in range(2)]
                    pT = [pst(st, f"pT{i}", [128, 8, 128], BF16) for i in range(2)]
                    kb.dma(g1[:], norm1_g[l:l + 1, :].to_broadcast([128, 1024]), [], ["g1"])
                    for r in range(2):
                        kb.dma(s_bc[r][:], mod_row(l, r, 1), [("modd", l)], [("s_bc", r)])
                        kb.dma(sh_bc[r][:], mod_row(l, r, 0), [("modd", l)], [("sh_bc", r)])
                        kb.stt(s_bc[r][:], s_bc[r][:], 1.0, g1[:], ALU.add, ALU.mult, [("s_bc", r), "g1"], [("s_bc", r)])
                    for t in range(NT):
                        b = t % 2
                        r = 0 if t < 4 else 1
                        kb.dma(xt[b][:], xsrc[t * 128:(t + 1) * 128, :], [(xskey, t)], [("xt", b)])
                        norm_tile(xt[b][:], ("xt", b), s_bc[r], sh_bc[r], [("s_bc", r), ("sh_bc", r)], junk, ss[b],
                                  tmp[b], hb[b], pT[b], hT[:, :, t * 128:(t + 1) * 128], ("hT", t), b)

                chk(1, l)

                def hTw(w):
                    return [("hT", t) for t in range(w * 4, w * 4 + 4)]

                with contextlib.ExitStack() as st:
                    wq = [sbt(st, f"wq{i}", [128, 8, 128], BF16) for i in range(2)]
                    wk = [sbt(st, f"wk{i}", [128, 8, 128], BF16) for i in range(2)]
                    wv = [sbt(st, f"wv{i}", [128, 8, 128], BF16) for i in range(2)]
                    qn = [sbt(st, f"qn{i}", [128, TOK], BF16) for i in range(2)]
                    kn = [sbt(st, f"kn{i}", [128, TOK], BF16) for i in range(2)]
                    Vz = [sbt(st, f"Vz{i}", [128, NT, 2, 128], BF16) for i in range(2)]
                    sq = [sbt(st, f"sq{i}", [128, 512], BF16) for i in range(2)]
                    rs = [sbt(st, f"rs{i}", [128, 512], F32) for i in range(2)]
                    kcT = sbt(st, "kcT", [128, 8, 256], BF16)
                    Vcz = sbt(st, "Vcz", [128, 2, 16, 128], BF16)
                    ckst = sbt(st, "ckst", [128, 1024], F32)
                    ckb = sbt(st, "ckb", [128, 1024], BF16)
                    TBraw = [sbt(st, f"TBraw{i}", [128, 22, 64], F32) for i in range(2)]
                    TBe = [sbt(st, "TBe0", [128, 22, 64], F32)] * 2
                    TB = [[sbt(st, f"TB{i}_{kd}", [128, 22, 64], BF16) for kd in range(2)] for i in range(4)]
                    masks = sbt(st, "masks", [128, 2, 22, 64], F32)
                    E = [sbt(st, f"E{i}", [128, 512], BF16) for i in range(8)]
                    rD = [sbt(st, f"rD{i}", [128, 512], F32) for i in range(2)]
                    naS = [sbt(st, f"naS{i}", [128, TOK], BF16) for i in range(2)]
                    kout = [sbt(st, "kout0", [128, 4, 128], F32)] * 2
                    vout = [sbt(st, "vout0", [128, 4, 128], F32)] * 2
                    pq = [pst(st, f"pq{i}", [128, 512], F32) for i in range(2)]
                    pss = pst(st, "pss", [128, 512], F32)
                    pS = [pst(st, f"pS{i}", [128, 512], F32) for i in range(2)]
                    pOx = [pst(st, "pOe", [128, 512], F32), pst(st, "pOo", [128, 512], F32)]
                    pTb = pst(st, "pTb", [128, 1024], BF16)

                    kb.dma(masks[:], c_mask, [], ["masks"])
                    for i in range(2):
                        kb.memset(TBraw[i][:], 0.0, [("TBraw", i)])
                        kb.memset(Vz[i][:], 1.0, [("Vz", i)])
                    kb.memset(Vcz[:], 1.0, ["Vcz"])
                    for kt in range(2):
                        kb.dma(ckst[:], ck[l, kt * 128:(kt + 1) * 128, :], [], ["ckst"])
                        kb.cp(ckb[:], ckst[:], ["ckst"], ["ckb"])
                        for hp in range(8):
                            kb.tr(pTb[:, hp * 128:(hp + 1) * 128], ckb[:, hp * 128:(hp + 1) * 128], ident_b[:],
                                  ["ckb", "ident_b"], ["pTb"])
                        kb.act(kcT[:, :, kt * 128:(kt + 1) * 128], pTb[:].rearrange("p (h t) -> p h t", h=8), AF.Copy,
                               ["pTb"], ["kcT"])
                    for kt in range(2):
                        kb.dma(ckst[:], cv[l, kt * 128:(kt + 1) * 128, :], [], ["ckst"])
                        src = ckst[:].rearrange("p (h e d) -> p h e d", e=2, d=64)
                        dstv = Vcz[:, kt].rearrange("p (h e) d -> p h e d", e=2)
                        for e in range(2):
                            kb.cp(dstv[:, :, e, e * 64:(e + 1) * 64], src[:, :, e, :], ["ckst"], ["Vcz"])


                    def win_items(n):
                        js = [range(0, 6), range(2, 10), range(6, 14), range(10, 16)][n]
                        out = []
                        for j in js:
                            i0 = 10 - 2 * j + 8 * n
                            if n == 0:
                                if j <= 3:
                                    segs, q = [(0, 4, 0), (4, 8, 1)], (0, 8)
                                else:
                                    segs, q = [(4, 8, 1)], (4, 8)
                            elif n == 3:
                                if j >= 12:
                                    segs, q = [(0, 5, 1), (5, 8, 0)], (0, 8)
                                else:
                                    segs, q = [(0, 5, 1)], (0, 5)
                            else:
                                segs, q = [(0, 8, 1)], (0, 8)
                            out.append((j, i0, q, segs))
                        return out

                    def prod_gen(hp):
                        b = hp % 2
                        kb.dma(wq[b][:], wview(w_in[l], Q0 + hp * 128, 128), [], [("wq", b)], q="pool")
                        kb.dma(wk[b][:], wview(w_in[l], K0 + hp * 128, 128), [], [("wk", b)], q="pool")
                        kb.dma(wv[b][:], wview(w_in[l], V0 + hp * 128, 128), [], [("wv", b)], q="pool")
                        for e in range(2):
                            h = 2 * hp + e
                            sl = b * 2 + e
                            kb.dma(TBraw[e][0:64, 3:18, :], rpbG[l, h], [], [("TBraw", e)])
                            kb.dma(TBraw[e][64:128, 4:19, :], rpbG[l, h], [], [("TBraw", e)])
                            kb.act(TBe[e][:], TBraw[e][:], AF.Exp, [("TBraw", e)], [("TBe", 0)])
                            for kd in range(2):
                                kb.tt(TB[sl][kd][:], TBe[e][:], masks[:, kd], ALU.mult, [("TBe", 0), "masks"], [("TB", sl)])
                            yield
                        tiles = [(wt, dst, gcol, key, w) for (wt, dst, gcol, key) in
                                 ((wq, qn, l * 2 + 0, "qn"), (wk, kn, l * 2 + 1, "kn")) for w in range(NW)]

                        def stepA(i):
                            wt, dst, gcol, key, w = tiles[i]
                            wkey = ("wq", b) if key == "qn" else ("wk", b)
                            pb = i % 2
                            for k in range(8):
                                kb.mm(pq[pb][:], wt[b][:, k, :], hT[:, k, w * 512:(w + 1) * 512], k == 0, k == 7,
                                      [wkey] + hTw(w), [("pq", pb)])
                            kb.act(sq[pb][:], pq[pb][:], AF.Square, [("pq", pb)], [("sq", pb)])

                        def stepB(i):
                            wt, dst, gcol, key, w = tiles[i]
                            pb = i % 2
                            kb.mm(pss[:], bd_b[:], sq[pb][:], True, True, [("sq", pb), "bd_b"], ["pss"])
                            kb.act(rs[pb][:], pss[:], AF.Ln, ["pss"], [("rs", pb)], scale=1.0 / 64, bias=EPS)
                            kb.act(rs[pb][:], rs[pb][:], AF.Exp, [("rs", pb)], [("rs", pb)], scale=-0.5)
                            kb.stt(dst[b][:, w * 512:(w + 1) * 512], pq[pb][:], gqk[:, gcol:gcol + 1], rs[pb][:],
                                   ALU.mult, ALU.mult, [("pq", pb), ("rs", pb), "gqk"], [(key, b, w)])

                        for i in range(len(tiles) + 1):
                            if i < len(tiles):
                                stepA(i)
                            if i >= 1:
                                stepB(i - 1)
                            yield
                        for t0 in range(0, NT, 4):
                            pb = (t0 // 4) % 2
                            for ti in range(4):
                                t = t0 + ti
                                for k in range(8):
                                    kb.mm(pq[pb][:, ti * 128:(ti + 1) * 128], hT[:, k, t * 128:(t + 1) * 128], wv[b][:, k, :],
                                          k == 0, k == 7, [("wv", b), ("hT", t)], [("pq", pb)])
                            base = Vz[b][:]
                            ps0 = base.ap[0][0]
                            dst = APc(base.tensor, base.offset + t0 * 256, [[ps0, 128], [256, 4], [192, 2], [1, 64]])
                            kb.act(dst, pq[pb][:].rearrange("p (t e d) -> p t e d", t=4, e=2), AF.Copy,
                                   [("pq", pb)], [("Vz", b)])
                            if t0 == 0:
                                kb.act(vout[b][:], pq[pb][:].rearrange("p (t d) -> p t d", t=4), AF.Copy, [("pq", pb)], [("vout", 0)])
                                for s in range(2):
                                    kb.dma(nv[s, l].rearrange("(t p) d -> p t d", p=128)[:, :, hp * 128:(hp + 1) * 128],
                                           vout[b][:, 2 * s:2 * s + 2, :], [("vout", 0)], [("nv", s, l, hp)])
                            yield
                        for t in range(4):
                            kb.tr(pTb[:, t * 128:(t + 1) * 128], kn[b][:, t * 128:(t + 1) * 128], ident_b[:],
                                  [("kn", b, 0), "ident_b"], ["pTb"])
                        kb.cp(kout[b][:], pTb[:, 0:512].rearrange("p (t d) -> p t d", t=4), ["pTb"], [("kout", 0)])
                        for s in range(2):
                            kb.dma(nk[s, l].rearrange("(t p) d -> p t d", p=128)[:, :, hp * 128:(hp + 1) * 128],
                                   kout[b][:, 2 * s:2 * s + 2, :], [("kout", 0)], [("nk", s, l, hp)])
                        yield

                    ecnt = 0
                    scnt = 0
                    for _ in prod_gen(0):
                        pass
                    for hp in range(8):
                        b = hp % 2
                        nxt = prod_gen(hp + 1) if hp < 7 else None
                        tickn = [0]

                        def tick():
                            tickn[0] += 1
                            if nxt is not None and tickn[0] % 4 == 0:
                                next(nxt, None)

                        citems = [(s_, e, kt) for s_ in range(2) for e in range(2) for kt in range(2)]
                        cst = [None] * len(citems)

                        def cS(ii):
                            nonlocal scnt, ecnt
                            s_, e, kt = citems[ii]
                            base = s_ * 256
                            hr = slice(e * 64, e * 64 + 64)
                            sb_ = scnt % 2
                            scnt += 1
                            eb = ecnt % 8
                            ecnt += 1
                            kb.mm(pS[sb_][:, 0:256], kn[b][hr, base + kt * 128:base + (kt + 1) * 128],
                                  qn[b][hr, base:base + 256], True, True, [("qn", b, 0), ("kn", b, 0)], [("pS", sb_)])
                            kb.act(E[eb][:, 0:256], pS[sb_][:, 0:256], AF.Exp, [("pS", sb_)], [("E", eb)])
                            cst[ii] = eb

                        def cPV(ii):
                            s_, e, kt = citems[ii]
                            base = s_ * 256
                            eb = cst[ii]
                            si = ii % 4
                            kb.mm(pOx[e][:, 0:256], Vz[b][:, 2 * s_ + kt, e, :], E[eb][:, 0:256], kt == 0, kt == 1,
                                  [("E", eb), ("Vz", b)], [("pO", e)])
                            if si == 3:
                                kb.act(rD[s_][0:64, 0:256], pOx[0][64:128, 0:256], AF.Ln, [("pO", 0)], [("rD", s_)])
                                kb.act(rD[s_][0:64, 0:256], rD[s_][0:64, 0:256], AF.Exp, [("rD", s_)], [("rD", s_)], scale=-1.0)
                                kb.tt(naS[b][0:64, base:base + 256], pOx[0][0:64, 0:256], rD[s_][0:64, 0:256], ALU.mult,
                                      [("pO", 0), ("rD", s_)], [("naS", b)])
                                kb.act(rD[s_][64:128, 0:256], pOx[1][0:64, 0:256], AF.Ln, [("pO", 1)], [("rD", s_)])
                                kb.act(rD[s_][64:128, 0:256], rD[s_][64:128, 0:256], AF.Exp, [("rD", s_)], [("rD", s_)], scale=-1.0)
                                kb.tt(naS[b][64:128, base:base + 256], pOx[1][64:128, 0:256], rD[s_][64:128, 0:256], ALU.mult,
                                      [("pO", 1), ("rD", s_)], [("naS", b)])

                        for ii in range(len(citems) + 2):
                            if ii < len(citems):
                                cS(ii)
                            if ii >= 2:
                                cPV(ii - 2)
                            tick()

                        for n in range(4):
                            qb = 512 + n * 512
                            per_e = []
                            for e in range(2):
                                lst = [(e, "c", 0)] + [(e, "w", it) for it in win_items(n)] + [(e, "c", 1)]
                                per_e.append(lst)
                            items = [x for pair in zip(per_e[0], per_e[1]) for x in pair]
                            first_of = {e: min(i for i, x in enumerate(items) if x[0] == e) for e in range(2)}
                            last_of = {e: max(i for i, x in enumerate(items) if x[0] == e) for e in range(2)}
                            LOOK = 4
                            st_ = [None] * len(items)

                            def stageS(ii):
                                nonlocal scnt, ecnt
                                e, kind, it = items[ii]
                                hr = slice(e * 64, e * 64 + 64)
                                sb_ = scnt % 2
                                scnt += 1
                                eb = ecnt % 8
                                ecnt += 1
                                if kind == "c":
                                    lk = kcT[hr, hp, it * 128:(it + 1) * 128]
                                    lkr = ["kcT"]
                                    qlo, qhi = 0, 512
                                    Vl = Vcz[:, it, 2 * hp + e, :]
                                    Vr = ["Vcz"]
                                else:
                                    j, i0, (qa, qz), segs = it
                                    lk = kn[b][hr, 512 + j * 128:512 + (j + 1) * 128]
                                    lkr = [("kn", b, 1 + j // 4)]
                                    qlo, qhi = qa * 64, qz * 64
                                    Vl = Vz[b][:, 4 + j, e, :]
                                    Vr = [("Vz", b)]
                                kb.mm(pS[sb_][:, qlo:qhi], lk, qn[b][hr, qb + qlo:qb + qhi], True, True,
                                      lkr + [("qn", b, 1 + n)], [("pS", sb_)])
                                kb.act(E[eb][:, qlo:qhi], pS[sb_][:, qlo:qhi], AF.Exp, [("pS", sb_)], [("E", eb)])
                                if kind == "w":
                                    for (a, z, kd) in segs:
                                        ev = E[eb][:, a * 64:z * 64].rearrange("p (a c) -> p a c", c=64)
                                        kb.tt(ev, ev, TB[b * 2 + e][kd][:, i0 + a:i0 + z, :], ALU.mult,
                                              [("E", eb), ("TB", b * 2 + e)], [("E", eb)])
                                st_[ii] = (e, eb, qlo, qhi, Vl, Vr)

                            def stagePV(ii):
                                e, eb, qlo, qhi, Vl, Vr = st_[ii]
                                first = ii == first_of[e]
                                last = ii == last_of[e]
                                kb.mm(pOx[e][:, qlo:qhi], Vl, E[eb][:, qlo:qhi], first, last, [("E", eb)] + Vr, [("pO", e)])

                            for ii in range(0, len(items) + LOOK, 2):
                                for jj in (ii, ii + 1):
                                    if jj < len(items):
                                        stageS(jj)
                                for jj in (ii, ii + 1):
                                    if LOOK <= jj < len(items) + LOOK:
                                        stagePV(jj - LOOK)
                                tick()
                                tick()
                            rb_ = n % 2
                            kb.act(rD[rb_][0:64, :], pOx[0][64:128, :], AF.Ln, [("pO", 0)], [("rD", rb_)])
                            kb.act(rD[rb_][0:64, :], rD[rb_][0:64, :], AF.Exp, [("rD", rb_)], [("rD", rb_)], scale=-1.0)
                            kb.tt(naS[b][0:64, qb:qb + 512], pOx[0][0:64, :], rD[rb_][0:64, :], ALU.mult,
                                  [("pO", 0), ("rD", rb_)], [("naS", b)])
                            kb.act(rD[rb_][64:128, :], pOx[1][0:64, :], AF.Ln, [("pO", 1)], [("rD", rb_)])
                            kb.act(rD[rb_][64:128, :], rD[rb_][64:128, :], AF.Exp, [("rD", rb_)], [("rD", rb_)], scale=-1.0)
                            kb.tt(naS[b][64:128, qb:qb + 512], pOx[1][64:128, :], rD[rb_][64:128, :], ALU.mult,
                                  [("pO", 1), ("rD", rb_)], [("naS", b)])
                        kb.dma(naT_d[hp], naS[b][:], [("naS", b)], [("naT_d", hp)])
                        if nxt is not None:
                            for _ in nxt:
                                pass

                chk(2, l)
                with contextlib.ExitStack() as st:
                    dt_t = sbt(st, "dt_t", [128, NT, 64], F32)
                    da_r = [sbt(st, f"da_r{i}", [128, 64], F32) for i in range(2)]
                    da_b16 = sbt(st, "da_b16", [128, NT, 64], BF16)
                    facin = sbt(st, "facin", [128, NT, 64], F32)
                    dd = sbt(st, "dd", [128, NT, 2, 64], F32)
                    ex1 = [sbt(st, f"ex1_{i}", [128, 64], F32) for i in range(2)]
                    wdt = sbt(st, "wdt", [128, 8, 64], BF16)
                    dtb_bc = sbt(st, "dtb_bc", [128, 64], F32)
                    a_bc = sbt(st, "a_bc", [128, 64], F32)
                    dsk_bc = sbt(st, "dsk_bc", [128, 32], F32)
                    t64 = [sbt(st, f"t64_{i}", [128, 64], F32) for i in range(2)]
                    cwt = sbt(st, "cwt", [128, 24, 4], F32)
                    cbt = sbt(st, "cbt", [128, 24], F32)
                    gnT = sbt(st, "gnT", [128, 16], F32)
                    ssq4 = sbt(st, "ssq4", [128, NT, 4], F32)
                    xs_tok = sbt(st, "xs_tok", [128, NT, 512], BF16)
                    B_tok = sbt(st, "B_tok", [128, NT, 128], BF16)
                    BT = sbt(st, "BT", [128, TOK], BF16)
                    CT = sbt(st, "CT", [128, TOK], BF16)
                    Hbe = sbt(st, "Hbe", [128, NT, 512], BF16)
                    Ucp2 = [sbt(st, f"Ucp{i}", [128, UCW], F32) for i in range(2)]
                    cvt = sbt(st, "cvt", [128, UCW], F32)
                    XT = [sbt(st, "XT0", [128, TOK], BF16)] * 2
                    wblk = [sbt(st, f"wblk{i}", [128, 8, 128], BF16) for i in range(2)]
                    wz = [sbt(st, "wz0", [128, 8, 512], BF16)] * 2
                    szc = [sbt(st, f"szc{i}", [128, 512], BF16) for i in range(2)]
                    DAm = [sbt(st, f"DAm{i}", [128, 4, 128], BF16) for i in range(2)]
                    LTt = [[sbt(st, f"LTt{i}_{j}", [128, 4, 128], BF16) for j in range(4)] for i in range(2)]
                    xdt = [[sbt(st, f"xdt{i}_{j}", [128, 512], BF16) for j in range(2)] for i in range(2)]
                    xwf = [sbt(st, f"xwf{i}", [128, 512], BF16) for i in range(2)]
                    t1 = [sbt(st, f"t1_{i}", [128, 512], BF16) for i in range(2)]
                    xw = t1
                    xsd = [sbt(st, f"xsd{i}", [128, 512], BF16) for i in range(2)]
                    yzb = [sbt(st, f"yzb{i}", [128, 512], BF16) for i in range(2)]
                    cbm = [sbt(st, f"cbm{i}", [128, 2, 128], BF16) for i in range(2)]
                    Hf = sbt(st, "Hf", [128, 512], F32)
                    Hb = sbt(st, "Hb", [128, 512], F32)
                    Hfb = sbt(st, "Hfb", [128, 512], BF16)
                    h0st = sbt(st, "h0st", [128, 4, 128], F32)
                    sto = [h0st] * 2
                    yzTs = [sbt(st, f"yzTs{i}", [128, 4, 128], BF16) for i in range(2)]
                    pp = [pst(st, f"pp{i}", [128, 512], F32) for i in range(2)]
                    pTb = pst(st, "pTbs", [128, 1024], BF16)
                    pseg = [pst(st, f"pseg{i}", [128, 4, 128], F32) for i in range(2)]
                    pY = pst(st, "pY", [128, 512], F32)
                    pZ = [pst(st, f"pZ{i}", [128, 512], F32) for i in range(2)]

                    for i in range(2):
                        kb.memset(Ucp2[i][:], 0.0, [("Ucp", i)])
                    kb.dma(cwt[:], cw_d[l], [], ["cwt"])
                    kb.dma(cbt[:], cb_d[l], [], ["cbt"])
                    kb.dma(gnT[:], gn_d[l], [], ["gnT"])
                    kb.dma(dtb_bc[:], dt_bias[l:l + 1, :].to_broadcast([128, 64]), [], ["dtb_bc"])
                    kb.dma(a_bc[:], a_log[l:l + 1, :].to_broadcast([128, 64]), [], ["a_bc"])
                    kb.dma(dsk_bc[:], d_skip[l:l + 1, :].to_broadcast([128, 32]), [], ["dsk_bc"])
                    kb.act(a_bc[:], a_bc[:], AF.Exp, ["a_bc"], ["a_bc"])
                    kb.ts(a_bc[:], a_bc[:], -1.0, None, ALU.mult, None, ["a_bc"], ["a_bc"])
                    kb.dma(wdt[:], wview(w_in[l], DT0, 64), [], ["wdt"], q="pool")
                    for t in range(NT):
                        b = t % 2
                        for k in range(8):
                            kb.mm(pp[b][:, 0:64], hT[:, k, t * 128:(t + 1) * 128], wdt[:, k, :], k == 0, k == 7,
                                  ["wdt", ("hT", t)], [("pp", b)])
                        kb.tt(t64[b][:], pp[b][:, 0:64], dtb_bc[:], ALU.add, [("pp", b), "dtb_bc"], [("t64", b)])
                        kb.act(t64[b][:], t64[b][:], AF.Exp, [("t64", b)], [("t64", b)])
                        kb.act(dt_t[:, t, :], t64[b][:], AF.Ln, [("t64", b)], [("dt", t)], bias=1.0)
                        kb.tt(da_r[b][:], dt_t[:, t, :], a_bc[:], ALU.mult, [("dt", t), "a_bc"], [("da", b)])
                        kb.cp(da_b16[:, t, :], da_r[b][:], [("da", b)], [("dab", t)])
                        pm3 = pp[b][:, 128:320].rearrange("p (a h) -> p a h", a=3)
                        kb.mm(pm3[:, 0, 0:32], tri_f[:, 0, :], da_r[b][:, 0:32], True, True, [("da", b), "tri_f"], [("pp", b)])
                        kb.mm(pm3[:, 0, 32:64], tri_f[:, 1, :], da_r[b][:, 32:64], True, True, [("da", b), "tri_f"], [("pp", b)])
                        kb.mm(pm3[:, 1, 0:32], tri_f[:, 2, :], da_r[b][:, 0:32], True, True, [("da", b), "tri_f"], [("pp", b)])
                        kb.mm(pm3[:, 1, 32:64], tri_f[:, 3, :], da_r[b][:, 32:64], True, True, [("da", b), "tri_f"], [("pp", b)])
                        kb.mm(pm3[:, 2, :], ones_f[:], da_r[b][:], True, True, [("da", b), "ones_f"], [("pp", b)])
                        kb.act(ex1[b][:], pm3[:, 0, :], AF.Exp, [("pp", b)], [("ex1", b)])
                        kb.act(dd[:, t], pm3[:, 1:3, :], AF.Exp, [("pp", b)], [("dd", t)])
                        kb.tt(facin[:, t, :], ex1[b][:], dt_t[:, t, :], ALU.mult, [("ex1", b), ("dt", t)], [("facin", t)])

                    def fm_block(wtile, wkey, ppi, ub):
                        Ucp = Ucp2[ub]
                        for w in range(NW):
                            pb = (ppi + w) % 2
                            for k in range(8):
                                kb.mm(pp[pb][:], wtile[:, k, :], hT[:, k, w * 512:(w + 1) * 512], k == 0, k == 7,
                                      [wkey] + hTw(w), [("pp", pb)])
                            if w == 0:
                                kb.act(Ucp[:, 2:258], pp[pb][:, 0:256], AF.Copy, [("pp", pb)], [("Ucp", ub)])
                                kb.act(Ucp[:, 260:516], pp[pb][:, 256:512], AF.Copy, [("pp", pb)], [("Ucp", ub)])
                            else:
                                c0 = 518 + (w - 1) * 512
                                kb.act(Ucp[:, c0:c0 + 512], pp[pb][:], AF.Copy, [("pp", pb)], [("Ucp", ub)])

                    def conv4(blk, ub):
                        Ucp = Ucp2[ub]
                        kb.ts(cvt[:, 2:2566], Ucp[:, 2:2566], cwt[:, blk, 2:3], cbt[:, blk:blk + 1], ALU.mult, ALU.add,
                              [("Ucp", ub), "cwt", "cbt"], ["cvt"])
                        for (kk, sh) in ((1, -1), (0, -2), (3, 1)):
                            kb.stt(cvt[:, 2:2566], Ucp[:, 2 + sh:2566 + sh], cwt[:, blk, kk:kk + 1], cvt[:, 2:2566],
                                   ALU.mult, ALU.add, [("Ucp", ub), "cwt", "cvt"], ["cvt"])

                    def silu_out(dst, dkey):
                        for (a, z, c0) in SEGCOL:
                            kb.act(dst[:, a:z], cvt[:, c0:c0 + (z - a)], AF.Silu, ["cvt"], [dkey])

                    wi = 0
                    zi = 0
                    ci_ = 0
                    for g in range(4):
                        blocks = [(X0 + g * 512 + bb * 128, "xs", bb, g * 4 + bb) for bb in range(4)]
                        blocks += [(B0 + g * 128, "B", 0, 16 + g), (C0 + g * 128, "C", 0, 20 + g)]
                        ubs = {}

                        def prodA1(idx):
                            nonlocal wi
                            (col0, kind, bb, blk) = blocks[idx]
                            wb_ = wi % 2
                            wi += 1
                            ubs[idx] = wi % 2
                            kb.dma(wblk[wb_][:], wview(w_in[l], col0, 128), [], [("wblk", wb_)], q="pool")
                            fm_block(wblk[wb_], ("wblk", wb_), wi, wi % 2)

                        def prodA2(idx):
                            (col0, kind, bb, blk) = blocks[idx]
                            conv4(blk, ubs[idx])

                        def prodS(idx):
                            (col0, kind, bb, blk) = blocks[idx]
                            if kind == "C":
                                silu_out(CT, "CT")
                            elif kind == "B":
                                silu_out(BT, "BT")
                            elif idx % 2 == 0:
                                silu_out(XT[0], ("XT", 0))
                            else:
                                silu_out(CT, "CT")

                        def prodB(idx):
                            (col0, kind, bb, blk) = blocks[idx]
                            if kind == "C":
                                return
                            if kind == "B":
                                src, skey = BT, "BT"
                            elif idx % 2 == 0:
                                src, skey = XT[0], ("XT", 0)
                            else:
                                src, skey = CT, "CT"
                            for t0 in (0, 8, 16):
                                nt = min(8, NT - t0)
                                for ti in range(nt):
                                    t = t0 + ti
                                    kb.tr(pTb[:, ti * 128:(ti + 1) * 128], src[:, t * 128:(t + 1) * 128], ident_b[:],
                                          [skey, "ident_b"], ["pTbs"])
                                if kind == "B":
                                    kb.act(B_tok[:, t0:t0 + nt, :], pTb[:, 0:nt * 128].rearrange("p (t c) -> p t c", c=128),
                                           AF.Copy, ["pTbs"], ["B_tok"])
                                else:
                                    kb.act(xs_tok[:, t0:t0 + nt, bb * 128:(bb + 1) * 128],
                                           pTb[:, 0:nt * 128].rearrange("p (t c) -> p t c", c=128), AF.Copy, ["pTbs"], ["xs_tok"])

                        for i_ in range(len(blocks) + 1):
                            if i_ < len(blocks):
                                prodA1(i_)
                            if i_ >= 1:
                                prodS(i_ - 1)
                            if i_ < len(blocks):
                                prodA2(i_)
                            if i_ >= 1:
                                prodB(i_ - 1)
                        zb = zi % 2
                        zi += 1
                        kb.dma(wz[zb][:], wview(w_in[l], Z0 + g * 512, 512), [], [("wz", 0)], q="pool")
                        hs = slice(g * 8, g * 8 + 8)
                        hsb = slice(32 + g * 8, 32 + g * 8 + 8)

                        def xs3(c):
                            return xs_tok[:, c, :].rearrange("p (h d) -> p h d", d=64)

                        def v3(ap):
                            return ap.rearrange("p (h d) -> p h d", d=64)

                        def load_state(srcd, Hdst, hk):
                            kb.dma(h0st[:], srcd[l, g * 512:(g + 1) * 512, :].rearrange("(b p) n -> p b n", p=128),
                                   [], ["h0st"])
                            for bb in range(4):
                                kb.mm(pZ[0][:, bb * 128:(bb + 1) * 128], h0st[:, bb, :], ident_f[:], True, True,
                                      ["h0st", "ident_f"], [("pZ", 0)])
                            kb.cp(Hdst[:], pZ[0][:], [("pZ", 0)], [hk])

                        def store_state(Hsrc, hk, dstd, s_):
                            for bb in range(4):
                                kb.mm(pZ[1][:, bb * 128:(bb + 1) * 128], Hsrc[:, bb * 128:(bb + 1) * 128], ident_f[:],
                                      True, True, [hk, "ident_f"], [("pZ", 1)])
                            stg = Ucp2[0][:, 1024:1536].rearrange("p (b n) -> p b n", b=4)
                            kb.cp(stg, pZ[1][:].rearrange("p (b n) -> p b n", b=4), [("pZ", 1)], [("Ucp", 0)])
                            kb.dma(dstd[s_, l, g * 512:(g + 1) * 512, :].rearrange("(b p) n -> p b n", p=128), stg,
                                   [("Ucp", 0)], [("nstate", hk, s_, l, g)])

                        for (s_, chunks) in SEQS:
                            if s_ < 2:
                                kb.memset(Hb[:], 0.0, ["Hb"])
                            else:
                                load_state(stb, Hb, "Hb")
                            for c in reversed(chunks):
                                xb = ci_ % 2
                                ci_ += 1
                                kb.act(Hbe[:, c, :], Hb[:], AF.Copy, ["Hb"], [("Hbe", c)])
                                kb.tt(v3(xw[xb][:]), xs3(c), bcast_heads(facin[:, c, hsb], 8, 64), ALU.mult,
                                      ["xs_tok", ("facin", c)], [("t1", xb)], eng="pool")
                                kb.mm(pZ[xb][:], B_tok[:, c, :], xw[xb][:], True, True, ["B_tok", ("t1", xb)], [("pZ", xb)])
                                Hb2 = Ucp2[1][:, 1024:1536]
                                kb.tt(v3(Hb2), v3(Hb[:]), bcast_heads(dd[:, c, 1, hsb], 8, 64), ALU.mult,
                                      ["Hb", ("dd", c)], [("Ucp", 1)])
                                kb.tt(Hb[:], Hb2, pZ[xb][:], ALU.add, [("Ucp", 1), ("pZ", xb)], ["Hb"])
                            if s_ < 2:
                                store_state(Hb, "Hb", nsb, s_)

                        seq_first = {chunks[0]: s_ for (s_, chunks) in SEQS}
                        seq_last = {chunks[-1]: s_ for (s_, chunks) in SEQS}
                        if True:
                            def stage1(c):
                                xb = c % 2
                                tokc = slice(c * 128, (c + 1) * 128)
                                for k in range(8):
                                    kb.mm(pp[xb][:], hT[:, k, tokc], wz[zb][:, k, :], k == 0, k == 7,
                                          [("wz", 0), ("hT", c)], [("pp", xb)])
                                kb.act(szc[xb][:], pp[xb][:], AF.Silu, [("pp", xb)], [("szc", xb)])
                                kb.mm(pZ[xb][:, 0:128], BT[:, tokc], CT[:, tokc], True, True, ["BT", "CT"], [("pZ", xb)])
                                kb.tt(cbm[xb][:], pZ[xb][:, 0:128].unsqueeze(1).to_broadcast([128, 2, 128]), tri_b[:, 2:4, :],
                                      ALU.mult, [("pZ", xb), "tri_b"], [("cbm", xb)])
                                for d in range(2):
                                    hsl = hs if d == 0 else hsb
                                    kb.tt(v3(xdt[xb][d][:]), xs3(c), bcast_heads(dt_t[:, c, hsl], 8, 64), ALU.mult,
                                          ["xs_tok", ("dt", c)], [("xdt", xb, d)], eng="pool")
                                kb.tt(v3(xsd[xb][:]), xs3(c), bcast_heads(dsk_bc[:, hs], 8, 64), ALU.mult,
                                      ["xs_tok", "dsk_bc"], [("xsd", xb)])
                                kb.tt(v3(xwf[xb][:]), xs3(c), bcast_heads(facin[:, c, hs], 8, 64), ALU.mult,
                                      ["xs_tok", ("facin", c)], [("xwf", xb)])
                                for d in range(2):
                                    for hq in range(2):
                                        sub = d * 2 + hq
                                        db = sub % 2
                                        h0 = d * 32 + g * 8 + hq * 4
                                        kb.tt(DAm[db][:], tri_b[:, d, :].unsqueeze(1).to_broadcast([128, 4, 128]),
                                              da_b16[:, c, h0:h0 + 4].unsqueeze(2).to_broadcast([128, 4, 128]), ALU.mult,
                                              ["tri_b", ("dab", c)], [("DAm", db)], eng="pool")
                                        for hh in range(4):
                                            kb.mm(pseg[db][:, hh, :], DAm[db][:, hh, :], tri_b[:, 2 + d, :], True, True,
                                                  [("DAm", db), "tri_b"], [("pseg", db)])
                                        kb.act(LTt[xb][sub][:], pseg[db][:], AF.Exp, [("pseg", db)], [("LTt", xb, sub)])
                                        kb.tt(LTt[xb][sub][:], LTt[xb][sub][:],
                                              cbm[xb][:, d, :].unsqueeze(1).to_broadcast([128, 4, 128]),
                                              ALU.mult, [("LTt", xb, sub), ("cbm", xb)], [("LTt", xb, sub)])

                            def stage2(c):
                                xb = c % 2
                                tokc = slice(c * 128, (c + 1) * 128)
                                if c in seq_first:
                                    if seq_first[c] < 2:
                                        kb.memset(Hf[:], 0.0, ["Hf"])
                                    else:
                                        load_state(stf, Hf, "Hf")
                                    kb.act(Hfb[:], Hf[:], AF.Copy, ["Hf"], ["Hfb"])
                                kb.mm(pZ[0][:], CT[:, tokc], Hfb[:], True, True, ["CT", "Hfb"], [("pZ", 0)])
                                kb.tt(v3(t1[0][:]), v3(pZ[0][:]), bcast_heads(dd[:, c, 0, hs], 8, 64), ALU.mult,
                                      [("pZ", 0), ("dd", c)], [("t1", 0)])
                                kb.mm(pZ[1][:], CT[:, tokc], Hbe[:, c, :], True, True, ["CT", ("Hbe", c)], [("pZ", 1)])
                                kb.tt(v3(t1[1][:]), v3(pZ[1][:]), bcast_heads(dd[:, c, 0, hsb], 8, 64), ALU.mult,
                                      [("pZ", 1), ("dd", c)], [("t1", 1)])
                                kb.mm(pY[:], ident_b[:], xsd[xb][:], True, False, [("xsd", xb), "ident_b"], ["pY"])
                                for d in range(2):
                                    for hq in range(2):
                                        sub = d * 2 + hq
                                        for hh in range(4):
                                            hc = (hq * 4 + hh) * 64
                                            kb.mm(pY[:, hc:hc + 64], LTt[xb][sub][:, hh, :], xdt[xb][d][:, hc:hc + 64], False, False,
                                                  [("LTt", xb, sub), ("xdt", xb, d)], ["pY"])
                                kb.mm(pY[:], ident_b[:], t1[0][:], False, False, [("t1", 0), "ident_b"], ["pY"])
                                kb.mm(pY[:], ident_b[:], t1[1][:], False, True, [("t1", 1), "ident_b"], ["pY"])
                                kb.mm(pZ[0][:], B_tok[:, c, :], xwf[xb][:], True, True, ["B_tok", ("xwf", xb)], [("pZ", 0)])
                                kb.tt(v3(Hf[:]), v3(Hf[:]), bcast_heads(dd[:, c, 1, hs], 8, 64), ALU.mult,
                                      ["Hf", ("dd", c)], ["Hf"])
                                kb.tt(Hf[:], Hf[:], pZ[0][:], ALU.add, ["Hf", ("pZ", 0)], ["Hf"])
                                kb.act(Hfb[:], Hf[:], AF.Copy, ["Hf"], ["Hfb"])
                                kb.tt(yzb[xb][:], pY[:], szc[xb][:], ALU.mult, ["pY", ("szc", xb)], [("yzb", xb)])
                                kb.act(t1[0][:], yzb[xb][:], AF.Square, [("yzb", xb)], [("t1", 0), ("ssq4", c)],
                                       accum=ssq4[:, c, g:g + 1])

                            def stage3(c):
                                xb = c % 2
                                for bb in range(4):
                                    kb.tr(pTb[:, bb * 128:(bb + 1) * 128], yzb[xb][:, bb * 128:(bb + 1) * 128], ident_b[:],
                                          [("yzb", xb), "ident_b"], ["pTbs"])
                                kb.act(yzTs[xb][:], pTb[:, 0:512].rearrange("p (b t) -> p b t", b=4), AF.Copy,
                                       ["pTbs"], [("yzTs", xb)])
                                kb.dma(yzT_d[g * 4:(g + 1) * 4, :, c * 128:(c + 1) * 128].rearrange("b p t -> p b t"),
                                       yzTs[xb][:], [("yzTs", xb)], [("yzT_d", g, c)])

                            for i in range(NT + 2):
                                if i < NT:
                                    stage1(i)
                                if 1 <= i <= NT:
                                    stage2(i - 1)
                                    if (i - 1) in seq_last and seq_last[i - 1] < 2:
                                        store_state(Hf, "Hf", nsf, seq_last[i - 1])
                                if i >= 2:
                                    stage3(i - 2)

                    for t in range(NT):
                        kb.tt(ssq4[:, t, 0:2], ssq4[:, t, 0:2], ssq4[:, t, 2:4], ALU.add, [("ssq4", t)], [("ssq4", t)])
                        kb.tt(ssq4[:, t, 0:1], ssq4[:, t, 0:1], ssq4[:, t, 1:2], ALU.add, [("ssq4", t)], [("ssq4", t)])
                        kb.act(ssq4[:, t, 1:2], ssq4[:, t, 0:1], AF.Ln, [("ssq4", t)], [("ssq4", t)], scale=1.0 / 2048, bias=EPS)
                        kb.act(rstd_tok[:, t:t + 1], ssq4[:, t, 1:2], AF.Exp, [("ssq4", t)], [("rstd_tok", t)], scale=-0.5)

                chk(3, l)
                with contextlib.ExitStack() as st:
                    wblk = [sbt(st, f"wgb{i}", [128, 8, 128], BF16) for i in range(2)]
                    gS = [sbt(st, f"gS{i}", [128, TOK], BF16) for i in range(2)]
                    pp = [pst(st, f"ppg{i}", [128, 512], F32) for i in range(4)]
                    pi = 0
                    g0 = ph0_gen(st, 1) if l == 0 else None
                    for gi in range(16):
                        b = gi % 2
                        col0 = (GA0 if gi < 8 else GB0) + (gi % 8) * 128
                        kb.dma(wblk[b][:], wview(w_in[l], col0, 128), [], [("wgb", b)], q="pool")
                        for w in range(NW):
                            pb = pi % 4
                            pi += 1
                            for k in range(8):
                                kb.mm(pp[pb][:], wblk[b][:, k, :], hT[:, k, w * 512:(w + 1) * 512], k == 0, k == 7,
                                      [("wgb", b)] + hTw(w), [("ppg", pb)])
                            kb.act(gS[b][:, w * 512:(w + 1) * 512], pp[pb][:], AF.Sigmoid, [("ppg", pb)], [("gS", b)])
                        kb.dma(gT_d[gi], gS[b][:], [("gS", b)], [("gT_d", gi)])
                        if g0 is not None:
                            next(g0, None)
                    if g0 is not None:
                        for _ in g0:
                            pass

                chk(4, l)
                Hst.close()
                with contextlib.ExitStack() as st:
                    wna = sbt(st, "wna", [128, 8, 1024], BF16)
                    wssd = sbt(st, "wssd", [128, 16, 1024], BF16)
                    wo = sbt(st, "wo", [128, 8, 1024], BF16)
                    naw = sbt(st, "naw", [128, 8, 512], BF16)
                    yzw = sbt(st, "yzw", [128, 16, 512], BF16)
                    gaw = sbt(st, "gaw", [128, 8, 512], BF16)
                    gbw = sbt(st, "gbw", [128, 8, 512], BF16)
                    mixT2 = [sbt(st, f"mixT{i}", [128, 8, 512], BF16) for i in range(2)]
                    rbc = sbt(st, "rbc", [128, 512], F32)
                    diag = [sbt(st, f"diag{i}", [128, 128], F32) for i in range(2)]
                    u1 = [sbt(st, f"u1_{i}", [128, 512], F32) for i in range(2)]
                    u2 = [sbt(st, f"u2_{i}", [128, 512], F32) for i in range(2)]
                    xt = [sbt(st, f"xt3_{i}", [128, 1024], F32) for i in range(2)]
                    tmp = [sbt(st, f"tmp3_{i}", [128, 1024], F32) for i in range(2)]
                    x1 = [sbt(st, f"x1_{i}", [128, 1024], F32) for i in range(2)]
                    hb = [sbt(st, f"hb3_{i}", [128, 1024], BF16) for i in range(2)]
                    junk = sbt(st, "junk3", [128, 1024], BF16)
                    ss = [sbt(st, f"ss3_{i}", [128, 4], F32) for i in range(2)]
                    g1_bc = [sbt(st, f"g1_bc{r}", [128, 1024], F32) for r in range(2)]
                    s2_bc = [sbt(st, f"s2_bc{r}", [128, 1024], F32) for r in range(2)]
                    sh2_bc = [sbt(st, f"sh2_bc{r}", [128, 1024], F32) for r in range(2)]
                    g2n = sbt(st, "g2n", [128, 1024], F32)
                    h2w = sbt(st, "h2w", [128, 8, 512], BF16)
                    pA = [pst(st, f"pA{i}", [128, 512], F32) for i in range(2)]
                    pB = [pst(st, f"pB{i}", [128, 512], F32) for i in range(2)]
                    pX = [pst(st, f"pX{i}", [128, 1024], F32) for i in range(1)]
                    pR = pst(st, "pR", [128, 512], F32)
                    pT = [pst(st, f"pT3_{i}", [128, 8, 128], BF16) for i in range(1)]

                    kb.dma(wna[:], wview(w_na_out[l], 0, 1024), [], ["wna"], q="pool")
                    kb.dma(wssd[:], wview(w_ssd_out[l], 0, 1024), [], ["wssd"], q="pool")
                    kb.dma(wo[:], wview(w_o[l], 0, 1024), [], ["wo"], q="pool")
                    gn3 = sbt(st, "gn3", [128, 16], F32)
                    kb.dma(gn3[:], gn_d[l], [], ["gn3"])
                    for k in range(16):
                        kb.ts(wssd[:, k, :], wssd[:, k, :], gn3[:, k:k + 1], None, ALU.mult, None, ["wssd", "gn3"], ["wssd"])
                    kb.dma(g2n[:], norm2_g[l:l + 1, :].to_broadcast([128, 1024]), [], ["g2n"])
                    for r in range(2):
                        kb.dma(g1_bc[r][:], mod_row(l, r, 2), [("modd", l)], [("g1_bc", r)])
                        kb.dma(s2_bc[r][:], mod_row(l, r, 4), [("modd", l)], [("s2_bc", r)])
                        kb.dma(sh2_bc[r][:], mod_row(l, r, 3), [("modd", l)], [("sh2_bc", r)])
                        kb.stt(s2_bc[r][:], s2_bc[r][:], 1.0, g2n[:], ALU.add, ALU.mult, [("s2_bc", r), "g2n"], [("s2_bc", r)])
                    ai = 0
                    def ph3_loads(w):
                        ws = slice(w * 512, (w + 1) * 512)
                        kb.dma(naw[:], naT_d[:, :, ws].rearrange("b p t -> p b t"), [("naT_d", hp) for hp in range(8)], ["naw"])
                        kb.dma(yzw[:], yzT_d[:, :, ws].rearrange("b p t -> p b t"),
                               [("yzT_d", g, c) for g in range(4) for c in range(w * 4, w * 4 + 4)], ["yzw"])
                        kb.dma(gaw[:], gT_d[0:8, :, ws].rearrange("b p t -> p b t"), [("gT_d", gi) for gi in range(8)], ["gaw"])
                        kb.dma(gbw[:], gT_d[8:16, :, ws].rearrange("b p t -> p b t"), [("gT_d", gi) for gi in range(8, 16)], ["gbw"])

                    def rstd_part(w):
                        for ti in range(4):
                            t = w * 4 + ti
                            db = ti % 2
                            kb.ts(diag[db][:], ident_f[:], rstd_tok[:, t:t + 1], None, ALU.mult, None,
                                  ["ident_f", ("rstd_tok", t)], [("diag", db)])
                            kb.mm(pR[:, ti * 128:(ti + 1) * 128], ones_f[:], diag[db][:], True, True,
                                  ["ones_f", ("diag", db)], ["pR"])
                        kb.act(rbc[:], pR[:], AF.Copy, ["pR"], ["rbc"])

                    def mix_part(w, cbk):
                        nonlocal ai
                        mixT = mixT2[w % 2]
                        ab = ai % 2
                        ai += 1
                        cs = slice(cbk * 128, (cbk + 1) * 128)
                        for k in range(8):
                            kb.mm(pA[ab][:], wna[:, k, cs], naw[:, k, :], k == 0, k == 7, ["wna", "naw"], [("pA", ab)])
                        for k in range(16):
                            kb.mm(pB[ab][:], wssd[:, k, cs], yzw[:, k, :], k == 0, k == 15, ["wssd", "yzw"], [("pB", ab)])
                        kb.tt(u1[ab][:], pA[ab][:], gaw[:, cbk, :], ALU.mult, [("pA", ab), "gaw"], [("u1", ab)])
                        kb.tt(u2[ab][:], pB[ab][:], rbc[:], ALU.mult, [("pB", ab), "rbc"], [("u2", ab)])
                        kb.tt(u2[ab][:], u2[ab][:], gbw[:, cbk, :], ALU.mult, [("u2", ab), "gbw"], [("u2", ab)])
                        kb.tt(mixT[:, cbk, :], u1[ab][:], u2[ab][:], ALU.add, [("u1", ab), ("u2", ab)], [("mixT", w % 2)])

                    def woA(w, ti):
                        mixT = mixT2[w % 2]
                        t = w * 4 + ti
                        b = t % 2
                        r = 0 if t < 4 else 1
                        for half in range(2):
                            for k in range(8):
                                kb.mm(pX[0][:, half * 512:(half + 1) * 512], mixT[:, k, ti * 128:(ti + 1) * 128],
                                      wo[:, k, half * 512:(half + 1) * 512], k == 0, k == 7, [("mixT", w % 2), "wo"], ["pX"])
                        kb.dma(xt[b][:], xsrc[t * 128:(t + 1) * 128, :], [(xskey, t)], [("xt3", b)])
                        kb.tt(tmp[b][:], pX[0][:], g1_bc[r][:], ALU.mult, ["pX", ("g1_bc", r)], [("tmp3", b)])
                        kb.tt(x1[b][:], tmp[b][:], xt[b][:], ALU.add, [("tmp3", b), ("xt3", b)], [("x1", b)], eng="pool")
                        kb.dma(x1d[t * 128:(t + 1) * 128, :], x1[b][:], [("x1", b)], [("x1d", t)])
                        kb.act(junk[:], x1[b][:], AF.Square, [("x1", b)], [("njunk",), ("nss", b)], accum=ss[b][:, 0:1])
                        kb.act(ss[b][:, 1:2], ss[b][:, 0:1], AF.Ln, [("nss", b)], [("nss", b)], scale=1.0 / 1024, bias=EPS)
                        kb.act(ss[b][:, 2:3], ss[b][:, 1:2], AF.Exp, [("nss", b)], [("nss", b)], scale=-0.5)
                        kb.stt(tmp[b][:], x1[b][:], ss[b][:, 2:3], s2_bc[r][:], ALU.mult, ALU.mult,
                               [("x1", b), ("nss", b), ("s2_bc", r)], [("tmp3", b)])
                        kb.tt(hb[b][:], tmp[b][:], sh2_bc[r][:], ALU.add, [("tmp3", b), ("sh2_bc", r)], [("nhb", b)])

                    def woB(w, ti):
                        t = w * 4 + ti
                        b = t % 2
                        for c in range(8):
                            kb.tr(pT[0][:, c, :], hb[b][:, c * 128:(c + 1) * 128], ident_b[:], [("nhb", b), "ident_b"], [("npT", 0)])
                        kb.act(h2w[:, :, ti * 128:(ti + 1) * 128], pT[0][:], AF.Copy, [("npT", 0)], ["h2w"])
                        if ti == 3:
                            kb.dma(h2T_d[:, :, w * 512:(w + 1) * 512].rearrange("b p t -> p b t"), h2w[:], ["h2w"], [("h2T_d", w)])

                    ph3_loads(0)
                    rstd_part(0)
                    for cbk in range(8):
                        mix_part(0, cbk)
                    if NW > 1:
                        ph3_loads(1)
                    for w in range(NW):
                        if w + 1 < NW:
                            rstd_part(w + 1)
                        for slot in range(8):
                            if w + 1 < NW:
                                mix_part(w + 1, slot)
                            if slot % 2 == 0:
                                woA(w, slot // 2)
                            else:
                                woB(w, slot // 2)
                        if w + 2 < NW:
                            ph3_loads(w + 2)

            chk(5, l)
            with contextlib.ExitStack() as Fst:
                actT = sbt(Fst, "actT", [128, 22, TOK], BF16)
                with contextlib.ExitStack() as st:
                    h2T = sbt(st, "h2T", [128, 8, TOK], BF16)
                    wvg = [sbt(st, f"wvg{i}", [128, 8, 128], BF16) for i in range(2)] * 2
                    UcpF2 = [sbt(st, f"UcpF{i}", [128, UCW], F32) for i in range(2)]
                    cval = sbt(st, "cval", [128, UCW], F32)
                    cgate = sbt(st, "cgate", [128, UCW], F32)
                    fwt = sbt(st, "fwt", [128, 44, 3], F32)
                    fbt = sbt(st, "fbt", [128, 44], F32)
                    pp = [pst(st, f"ppf{i}", [128, 512], F32) for i in range(4)]
                    for i in range(2):
                        kb.memset(UcpF2[i][:], 0.0, [("UcpF", i)])
                    kb.dma(fwt[:], fw_d[l], [], ["fwt"])
                    kb.dma(fbt[:], fb_d[l], [], ["fbt"])
                    for w in range(NW):
                        kb.dma(h2T[:, :, w * 512:(w + 1) * 512], h2T_d[:, :, w * 512:(w + 1) * 512].rearrange("b p t -> p b t"),
                               [("h2T_d", w)], [("h2T", w)])
                    pi = 0
                    wi = 0
                    def ffA1(j, half):
                        nonlocal wi, pi
                        wb_ = wi % 2
                        ub = wi % 2
                        Ucp = UcpF2[ub]
                        wi += 1
                        kb.dma(wvg[wb_][:], wview(w_up[l], half * DFF + j * 128, 128), [], [("wvg", wb_)], q="pool")
                        for w in range(NW):
                            pb = pi % 4
                            pi += 1
                            for k in range(8):
                                kb.mm(pp[pb][:], wvg[wb_][:, k, :], h2T[:, k, w * 512:(w + 1) * 512], k == 0, k == 7,
                                      [("wvg", wb_), ("h2T", w)], [("ppf", pb)])
                            if w == 0:
                                kb.act(Ucp[:, 2:258], pp[pb][:, 0:256], AF.Copy, [("ppf", pb)], [("UcpF", ub)])
                                kb.act(Ucp[:, 260:516], pp[pb][:, 256:512], AF.Copy, [("ppf", pb)], [("UcpF", ub)])
                            else:
                                c0 = 518 + (w - 1) * 512
                                kb.act(Ucp[:, c0:c0 + 512], pp[pb][:], AF.Copy, [("ppf", pb)], [("UcpF", ub)])
                        return ub

                    def ffA2(j, half, ub):
                        Ucp = UcpF2[ub]
                        blk = half * 22 + j
                        dst, dk = (cval, "cval") if half == 0 else (cgate, "cgate")
                        kb.ts(dst[:, 2:2566], Ucp[:, 2:2566], fwt[:, blk, 1:2], fbt[:, blk:blk + 1], ALU.mult, ALU.add,
                              [("UcpF", ub), "fwt", "fbt"], [dk])
                        for (kk, sh) in ((0, -1), (2, 1)):
                            kb.stt(dst[:, 2:2566], Ucp[:, 2 + sh:2566 + sh], fwt[:, blk, kk:kk + 1], dst[:, 2:2566],
                                   ALU.mult, ALU.add, [("UcpF", ub), "fwt", dk], [dk])

                    def ffS(j):
                        kb.act(cgate[:, 2:2566], cgate[:, 2:2566], AF.Silu, ["cgate"], ["cgate"])
                        for (a, z, c0) in SEGCOL:
                            kb.tt(actT[:, j, a:z], cval[:, c0:c0 + (z - a)], cgate[:, c0:c0 + (z - a)], ALU.mult,
                                  ["cval", "cgate"], [("actT", j)])

                    for j in range(23):
                        if j < 22:
                            ub0 = ffA1(j, 0)
                        if j >= 1:
                            ffS(j - 1)
                        if j < 22:
                            ffA2(j, 0, ub0)
                            ub1 = ffA1(j, 1)
                            ffA2(j, 1, ub1)

                chk(6, l)
                with contextlib.ExitStack() as st:
                    wdn = sbt(st, "wdn", [128, 22, 1024], BF16)
                    g2_bc = [sbt(st, f"g2_bc{r}", [128, 1024], F32) for r in range(2)]
                    xt = [sbt(st, f"xt5_{i}", [128, 1024], F32) for i in range(2)]
                    tmp = [sbt(st, f"tmp5_{i}", [128, 1024], F32) for i in range(2)]
                    x2 = [sbt(st, f"x2_{i}", [128, 1024], F32) for i in range(2)]
                    pX = [pst(st, f"pX5_{i}", [128, 1024], F32) for i in range(2)]
                    for kq in range(2):
                        kb.dma(wdn[:, kq * 11:(kq + 1) * 11, :],
                               w_down[l].rearrange("(k p) c -> p k c", p=128)[:, kq * 11:(kq + 1) * 11, :], [], [("wdn", kq)],
                               q="pool")
                    for r in range(2):
                        kb.dma(g2_bc[r][:], mod_row(l, r, 5), [("modd", l)], [("g2_bc", r)])
                    actk = [("actT", j) for j in range(22)]
                    for t in range(NT):
                        b = t % 2
                        r = 0 if t < 4 else 1
                        for half in range(2):
                            for k in range(22):
                                kb.mm(pX[b][:, half * 512:(half + 1) * 512], actT[:, k, t * 128:(t + 1) * 128],
                                      wdn[:, k, half * 512:(half + 1) * 512], k == 0, k == 21,
                                      actk + [("wdn", 0), ("wdn", 1)], [("pX5", b)])
                        kb.dma(xt[b][:], x1d[t * 128:(t + 1) * 128, :], [("x1d", t)], [("xt5", b)])
                        kb.tt(tmp[b][:], pX[b][:], g2_bc[r][:], ALU.mult, [("pX5", b), ("g2_bc", r)], [("tmp5", b)])
                        kb.tt(x2[b][:], tmp[b][:], xt[b][:], ALU.add, [("tmp5", b), ("xt5", b)], [("x2", b)])
                        kb.dma(xfin[t * 128:(t + 1) * 128, :], x2[b][:], [("x2", b)], [(xfkey, t)])

            chk(7, l)
        P.emit(nc)
    return kb


def _consts():
    c = {}
    c["c_ident"] = np.eye(128, dtype=np.float32)
    t = np.arange(128)[:, None]
    j = np.arange(128)[None, :]
    c["c_tri"] = np.stack([(t > j), (t < j), (t <= j), (t >= j)], axis=1).astype(np.float32)
    bd = np.zeros((128, 128), np.float32)
    bd[:64, :64] = 1.0
    bd[64:, 64:] = 1.0
    c["c_bd"] = bd
    oz = np.zeros((128, 2, 128), np.float32)
    oz[:, 0, :64] = 1.0
    oz[:, 1, 64:] = 1.0
    c["c_onesz"] = oz
    cols = np.arange(64)
    cs = np.clip(cols - 8, 0, 48)
    cm = (cols[None, :] >= cs[:, None]) & (cols[None, :] < cs[:, None] + 16)
    cmT = cm.T.astype(np.float32)
    m = np.zeros((128, 2, 22, 64), np.float32)
    for par in range(2):
        for i in range(22):
            dr = 10 + par - i
            if -7 <= dr <= 7:
                m[par * 64:(par + 1) * 64, 0, i, :] = cmT
            if -4 <= dr <= 3:
                m[par * 64:(par + 1) * 64, 1, i, :] = cmT
    c["c_mask"] = m
    return c


_CACHE = {}
NCORES_RUN = 8


def kernel(x_prompt, x_sample, c, cache_k, cache_v, state_ssd_fwd, state_ssd_bwd, c_ctx,
           w_ada, b_ada, norm1_g, w_in, q_norm_g, k_norm_g, rpb, ssd_conv_w, ssd_conv_b,
           a_log, dt_bias, d_skip, ssd_norm_g, w_na_out, w_ssd_out, w_o, norm2_g,
           w_up, ffn_conv_w, ffn_conv_b, w_down):
    f = lambda a: np.ascontiguousarray(np.asarray(a, dtype=np.float32))
    x_prompt, x_sample, c, cache_k, cache_v = f(x_prompt), f(x_sample), f(c), f(cache_k), f(cache_v)
    state_ssd_fwd, state_ssd_bwd, c_ctx = f(state_ssd_fwd), f(state_ssd_bwd), f(c_ctx)
    if "kb" not in _CACHE:
        _CACHE["kb"] = build()
    kb = _CACHE["kb"]

    shared = dict(_consts())
    shared["w_ada"] = f(w_ada)
    shared["b_ada"] = f(b_ada)
    shared["norm1_g"] = f(norm1_g)
    shared["w_in"] = f(w_in)
    qg, kg = f(q_norm_g), f(k_norm_g)
    gqk = np.zeros((128, 4), np.float32)
    for l in range(DEPTH):
        gqk[:, 2 * l + 0] = np.tile(qg[l], 2)
        gqk[:, 2 * l + 1] = np.tile(kg[l], 2)
    shared["gqk"] = gqk
    kc = np.arange(64)[:, None]
    qc = np.arange(64)[None, :]
    cidx = np.clip(kc - qc + 15, 0, 30)
    rp = f(rpb)[:, :, ::-1, :]
    G = rp[:, :, :, cidx]
    shared["rpbG"] = np.ascontiguousarray(np.transpose(G, (0, 1, 3, 2, 4)))
    shared["cw"] = np.ascontiguousarray(f(ssd_conv_w).reshape(DEPTH, 4, 24, 128).transpose(0, 3, 2, 1))
    shared["cb"] = np.ascontiguousarray(f(ssd_conv_b).reshape(DEPTH, 24, 128).transpose(0, 2, 1))
    shared["a_log"] = f(a_log).reshape(DEPTH, 64)
    shared["dt_bias"] = f(dt_bias).reshape(DEPTH, 64)
    shared["d_skip"] = f(d_skip)
    shared["gn"] = np.ascontiguousarray(f(ssd_norm_g).reshape(DEPTH, 16, 128).transpose(0, 2, 1))
    shared["w_na_out"] = f(w_na_out)
    shared["w_ssd_out"] = f(w_ssd_out)
    shared["w_o"] = f(w_o)
    shared["norm2_g"] = f(norm2_g)
    shared["w_up"] = f(w_up)
    shared["fw"] = np.ascontiguousarray(f(ffn_conv_w).reshape(DEPTH, 3, 44, 128).transpose(0, 3, 2, 1))
    shared["fb"] = np.ascontiguousarray(f(ffn_conv_b).reshape(DEPTH, 44, 128).transpose(0, 2, 1))
    shared["w_down"] = f(w_down)

    in_maps = []
    for i in range(NCORES_RUN):
        m = dict(shared)
        m["xin"] = np.concatenate([x_prompt[2 * i].reshape(256, 1024), x_prompt[2 * i + 1].reshape(256, 1024),
                                   x_sample[i].reshape(2048, 1024)], axis=0)
        cvec = np.stack([c_ctx, c[i]], axis=0)
        m["cvecT"] = np.ascontiguousarray(cvec.reshape(2, 8, 128).transpose(2, 1, 0))
        m["ck"] = np.ascontiguousarray(cache_k[i].reshape(DEPTH, 256, 1024))
        m["cv"] = np.ascontiguousarray(cache_v[i].reshape(DEPTH, 256, 1024))
        m["stf"] = np.ascontiguousarray(state_ssd_fwd[i].reshape(DEPTH, 2048, 128))
        m["stb"] = np.ascontiguousarray(state_ssd_bwd[i].reshape(DEPTH, 2048, 128))
        for k_, shp in kb.ins.items():
            assert tuple(m[k_].shape) == tuple(shp), (k_, m[k_].shape, shp)
        in_maps.append(m)

    if NCORES_RUN != 8:
        return run_bass_kernel_spmd(kb.nc, in_maps, core_ids=list(range(NCORES_RUN))).results
    res = run_bass_kernel_spmd(kb.nc, in_maps, core_ids=list(range(8)))
    R = res.results
    y_prompt = np.zeros((16, 256, 1024), np.float32)
    y_sample = np.zeros((8, 2048, 1024), np.float32)
    nck = np.zeros((16, DEPTH, 256, 16, 64), np.float32)
    ncv = np.zeros((16, DEPTH, 256, 16, 64), np.float32)
    nsf = np.zeros((16, DEPTH, 32, 64, 128), np.float32)
    nsb = np.zeros((16, DEPTH, 32, 64, 128), np.float32)
    for i in range(8):
        yo = np.asarray(R[i]["yout"])
        y_prompt[2 * i] = yo[0:256]
        y_prompt[2 * i + 1] = yo[256:512]
        y_sample[i] = yo[512:]
        for s in range(2):
            nck[2 * i + s] = np.asarray(R[i]["nk"])[s].reshape(DEPTH, 256, 16, 64)
            ncv[2 * i + s] = np.asarray(R[i]["nv"])[s].reshape(DEPTH, 256, 16, 64)
            nsf[2 * i + s] = np.asarray(R[i]["nsf"])[s].reshape(DEPTH, 32, 64, 128)
            nsb[2 * i + s] = np.asarray(R[i]["nsb"])[s].reshape(DEPTH, 32, 64, 128)
    return (y_prompt, y_sample, nck, ncv, nsf, nsb)
```
